# Optimizing a Trainium2 kernel written in Bass

```python
import math
import jax, jax.numpy as jnp
from jax import lax
import numpy as np

D_MODEL = 1024
BATCH = 32
SEQ = 2048
DEPTH = 4

N_MIXERS = 2
N_GLA = (DEPTH + 1) // 2
N_MLA = DEPTH // 2
EXPAND = 2
D_BRANCH = EXPAND * D_MODEL
GLA_HEADS = 4
GLA_DK = (D_MODEL // 2) // GLA_HEADS
GLA_DV = D_BRANCH // GLA_HEADS
GLA_GATE_RANK = 16
GLA_TAU = 16.0
GLA_CHUNK = 64
GLA_IN = 2 * GLA_HEADS * GLA_DK + 2 * D_BRANCH + 2 * GLA_GATE_RANK
MLA_HEADS = 16
MLA_Q_RANK = 384
MLA_KV_RANK = 256
MLA_NOPE = 128
MLA_ROPE = 64
MLA_DV = D_BRANCH // MLA_HEADS
MLA_IN = MLA_Q_RANK + MLA_KV_RANK + MLA_ROPE + D_BRANCH
ROPE_BASE = 10000.0
Q_BLOCK = 128
ALPHA = (2 * DEPTH) ** 0.25
BETA = (8 * DEPTH) ** -0.25
EPS = 1e-5

kernel_name = "hybrid_gla_mla_deepnorm_encoder"


def layer_norm(x, g, b):
    xf = x.astype(jnp.float32)
    mu = jnp.mean(xf, axis=-1, keepdims=True)
    var = jnp.mean(jnp.square(xf - mu), axis=-1, keepdims=True)
    return ((xf - mu) * lax.rsqrt(var + EPS) * g + b).astype(x.dtype)


def rms_norm(x, g):
    xf = x.astype(jnp.float32)
    return (xf * lax.rsqrt(jnp.mean(jnp.square(xf), axis=-1, keepdims=True) + EPS) * g).astype(x.dtype)


def rope_tables(positions):
    inv_freq = 1.0 / (ROPE_BASE ** (jnp.arange(0, MLA_ROPE, 2, dtype=jnp.float32) / MLA_ROPE))
    ang = positions.astype(jnp.float32)[..., None] * inv_freq
    return jnp.cos(ang), jnp.sin(ang)


def apply_rope(x, cos, sin):
    xf = x.astype(jnp.float32)
    x1, x2 = xf[..., : MLA_ROPE // 2], xf[..., MLA_ROPE // 2:]
    return jnp.concatenate([x1 * cos - x2 * sin, x2 * cos + x1 * sin], axis=-1).astype(x.dtype)


def gla_scan(q, k, v, lg):
    B, S, H, DK = q.shape
    DV = v.shape[-1]
    n = S // GLA_CHUNK

    def to_chunks(t):
        return t.reshape(B, n, GLA_CHUNK, H, t.shape[-1]).transpose(1, 0, 3, 2, 4)

    mask = jnp.tril(jnp.ones((GLA_CHUNK, GLA_CHUNK), dtype=bool))[None, None, :, :, None]

    def step(state, inp):
        qc, kc, vc, gc = inp
        b = jnp.cumsum(gc, axis=2)
        b_end = b[:, :, -1:, :]
        o_inter = jnp.einsum('bhcd,bhde->bhce', qc * jnp.exp(b), state)
        diff = b[:, :, :, None, :] - b[:, :, None, :, :]
        decay = jnp.where(mask, jnp.exp(jnp.where(mask, diff, 0.0)), 0.0)
        scores = jnp.einsum('bhid,bhjd,bhijd->bhij', qc, kc, decay)
        o_intra = jnp.einsum('bhij,bhje->bhie', scores, vc)
        new_state = (jnp.exp(b_end)[:, :, 0, :, None] * state
                     + jnp.einsum('bhcd,bhce->bhde', kc * jnp.exp(b_end - b), vc))
        return new_state, o_inter + o_intra

    state0 = jnp.zeros((B, H, DK, DV), jnp.float32)
    _, o = lax.scan(step, state0, (to_chunks(q), to_chunks(k), to_chunks(v), to_chunks(lg)))
    return o.transpose(1, 0, 3, 2, 4).reshape(B, S, H, DV)


def gla_mixer(x, w_in, w_gate, b_gate, gn_g, w_out):
    B, S, _ = x.shape
    dqk = GLA_HEADS * GLA_DK
    h = x @ w_in
    q, k, v, z, glr = jnp.split(h, [dqk, 2 * dqk, 2 * dqk + D_BRANCH, 2 * dqk + 2 * D_BRANCH], axis=-1)
    q = q.reshape(B, S, GLA_HEADS, GLA_DK) * (GLA_DK ** -0.5)
    k = k.reshape(B, S, GLA_HEADS, GLA_DK)
    v = v.reshape(B, S, GLA_HEADS, GLA_DV)
    glr = glr.reshape(B, S, 2, GLA_GATE_RANK)
    pre = jnp.einsum('bsnr,nrk->bsnk', glr, w_gate) + b_gate
    lg = (jax.nn.log_sigmoid(pre.astype(jnp.float32)) / GLA_TAU).reshape(B, S, 2, GLA_HEADS, GLA_DK)
    o_fw = gla_scan(q, k, v, lg[:, :, 0])
    o_bw = jnp.flip(gla_scan(jnp.flip(q, 1), jnp.flip(k, 1), jnp.flip(v, 1), jnp.flip(lg[:, :, 1], 1)), 1)
    o = (o_fw + o_bw).astype(jnp.float32)
    o = o * lax.rsqrt(jnp.mean(jnp.square(o), axis=-1, keepdims=True) + EPS)
    o = (o.reshape(B, S, D_BRANCH) * gn_g).astype(x.dtype)
    return (o * jax.nn.silu(z)) @ w_out


def mla_mixer(x, cos, sin, w_in, q_norm_g, kv_norm_g, w_uq, w_ukv, w_out):
    B, S, _ = x.shape
    h = x @ w_in
    cq, ckv, kr, z = jnp.split(h, [MLA_Q_RANK, MLA_Q_RANK + MLA_KV_RANK,
                                   MLA_Q_RANK + MLA_KV_RANK + MLA_ROPE], axis=-1)
    q = (rms_norm(cq, q_norm_g) @ w_uq).reshape(B, S, MLA_HEADS, MLA_NOPE + MLA_ROPE)
    qn = q[..., :MLA_NOPE]
    qr = apply_rope(q[..., MLA_NOPE:], cos[:, :, None], sin[:, :, None])
    kv = (rms_norm(ckv, kv_norm_g) @ w_ukv).reshape(B, S, MLA_HEADS, MLA_NOPE + MLA_DV)
    kn, v = kv[..., :MLA_NOPE], kv[..., MLA_NOPE:]
    kr = apply_rope(kr, cos, sin)
    scale = (MLA_NOPE + MLA_ROPE) ** -0.5
    nb = S // Q_BLOCK
    qn_b = qn.reshape(B, nb, Q_BLOCK, MLA_HEADS, MLA_NOPE).transpose(1, 0, 2, 3, 4)
    qr_b = qr.reshape(B, nb, Q_BLOCK, MLA_HEADS, MLA_ROPE).transpose(1, 0, 2, 3, 4)

    def attend(blk):
        qn_i, qr_i = blk
        s = jnp.einsum('bqhd,bkhd->bhqk', qn_i, kn) + jnp.einsum('bqhd,bkd->bhqk', qr_i, kr)
        p = jax.nn.softmax(s.astype(jnp.float32) * scale, axis=-1)
        return jnp.einsum('bhqk,bkhd->bqhd', p.astype(v.dtype), v)

    o = lax.map(attend, (qn_b, qr_b))
    o = o.transpose(1, 0, 2, 3, 4).reshape(B, S, D_BRANCH)
    return (o * jax.nn.silu(z)) @ w_out


def setup_inputs(seed: int = 0) -> dict:
    key = jax.random.key(seed)
    ks = jax.random.split(key, 16)
    nrm = jax.random.normal
    f32 = jnp.float32
    x = nrm(ks[0], (BATCH, SEQ, D_MODEL), f32)
    offsets = jax.random.randint(ks[1], (BATCH, 1), 0, 4096, dtype=jnp.int32)
    positions = offsets + jnp.arange(SEQ, dtype=jnp.int32)[None, :]
    ln_g = 1.0 + 0.02 * nrm(ks[2], (DEPTH, D_MODEL), f32)
    ln_b = 0.02 * nrm(ks[3], (DEPTH, D_MODEL), f32)
    gla_w_in = nrm(ks[4], (N_GLA, D_MODEL, GLA_IN), f32) * D_MODEL ** -0.5
    gla_w_gate = nrm(ks[5], (N_GLA, 2, GLA_GATE_RANK, GLA_HEADS * GLA_DK), f32) * GLA_GATE_RANK ** -0.5
    gla_b_gate = 0.5 * nrm(ks[6], (N_GLA, 2, GLA_HEADS * GLA_DK), f32)
    gla_gn_g = 1.0 + 0.02 * nrm(ks[7], (N_GLA, D_BRANCH), f32)
    gla_w_out = nrm(ks[8], (N_GLA, D_BRANCH, D_MODEL), f32) * (D_BRANCH ** -0.5) * BETA
    mla_w_in = nrm(ks[9], (N_MLA, D_MODEL, MLA_IN), f32) * D_MODEL ** -0.5
    mla_q_norm_g = 1.0 + 0.02 * nrm(ks[10], (N_MLA, MLA_Q_RANK), f32)
    mla_kv_norm_g = 1.0 + 0.02 * nrm(ks[11], (N_MLA, MLA_KV_RANK), f32)
    mla_w_uq = nrm(ks[12], (N_MLA, MLA_Q_RANK, MLA_HEADS * (MLA_NOPE + MLA_ROPE)), f32) * MLA_Q_RANK ** -0.5
    mla_w_ukv = nrm(ks[13], (N_MLA, MLA_KV_RANK, MLA_HEADS * (MLA_NOPE + MLA_DV)), f32) * MLA_KV_RANK ** -0.5
    mla_w_out = nrm(ks[14], (N_MLA, D_BRANCH, D_MODEL), f32) * (D_BRANCH ** -0.5) * BETA
    return {"x": x, "positions": positions, "ln_g": ln_g, "ln_b": ln_b,
            "gla_w_in": gla_w_in, "gla_w_gate": gla_w_gate, "gla_b_gate": gla_b_gate,
            "gla_gn_g": gla_gn_g, "gla_w_out": gla_w_out,
            "mla_w_in": mla_w_in, "mla_q_norm_g": mla_q_norm_g, "mla_kv_norm_g": mla_kv_norm_g,
            "mla_w_uq": mla_w_uq, "mla_w_ukv": mla_w_ukv, "mla_w_out": mla_w_out}


def reference(x, positions, ln_g, ln_b, gla_w_in, gla_w_gate, gla_b_gate, gla_gn_g, gla_w_out,
              mla_w_in, mla_q_norm_g, mla_kv_norm_g, mla_w_uq, mla_w_ukv, mla_w_out):
    cos, sin = rope_tables(positions)
    for i in range(DEPTH):
        j = i // N_MIXERS
        if i % N_MIXERS == 0:
            y = gla_mixer(x, gla_w_in[j], gla_w_gate[j], gla_b_gate[j], gla_gn_g[j], gla_w_out[j])
        else:
            y = mla_mixer(x, cos, sin, mla_w_in[j], mla_q_norm_g[j], mla_kv_norm_g[j],
                          mla_w_uq[j], mla_w_ukv[j], mla_w_out[j])
        x = layer_norm(ALPHA * x + y.astype(x.dtype), ln_g[i], ln_b[i])
    return x
```

```python
import math
from contextlib import ExitStack

import numpy as np
import concourse.bass as bass
import concourse.mybir as mybir
from concourse.bass_utils import run_bass_kernel_spmd

F32 = mybir.dt.float32
BF16 = mybir.dt.bfloat16
I32 = mybir.dt.int32
AF = mybir.ActivationFunctionType
ALU = mybir.AluOpType

ENGS = ('pe', 'act', 'dve', 'pool', 'sp')

S = 2048
D = 1024
NT = 16
KC = 8
DEPTH = 4
ALPHA = (2 * DEPTH) ** 0.25
EPS = 1e-5
NSEQ = 4
NCORES = 8


class Op:
    __slots__ = ('eng', 'fn', 'deps', 'sig', 'cnt', 'dma', 'dsem', 'dval', 'prev_dma')


class Prog:
    def __init__(self, nc, n_dma_sems=16, same_engine_sync=True):
        self.nc = nc
        self.ops = {e: [] for e in ENGS}
        self.last_w = {}
        self.readers = {}
        self.n_dma_sems = n_dma_sems
        self.dma_count = 0
        self.dma_hist = [[] for _ in range(n_dma_sems)]
        self.same_engine_sync = same_engine_sync

    def op(self, eng, fn, reads=(), writes=(), dma=False):
        o = Op()
        o.eng = eng
        o.fn = fn
        o.sig = False
        o.cnt = None
        o.dma = dma
        o.dsem = None
        o.dval = None
        o.prev_dma = None
        deps = []
        seen = set()

        def add(d, raw):
            if d is None or id(d) in seen:
                return
            if (not d.dma) and (not dma) and d.eng == eng:
                if eng == 'pe' or not self.same_engine_sync or not raw:
                    return
            seen.add(id(d))
            deps.append(d)

        for r in reads:
            add(self.last_w.get(r), True)
        for w in writes:
            add(self.last_w.get(w), False)
            for rd in self.readers.get(w, ()):
                add(rd, False)
        o.deps = deps
        for d in deps:
            if not d.dma:
                d.sig = True
        if dma:
            k = self.dma_count % self.n_dma_sems
            self.dma_count += 1
            hist = self.dma_hist[k]
            o.prev_dma = hist[-1] if hist else None
            hist.append(o)
            o.dsem = k
            o.dval = 16 * len(hist)
        for r in reads:
            self.readers.setdefault(r, []).append(o)
        for w in writes:
            self.last_w[w] = o
            self.readers[w] = []
        self.ops[eng].append(o)
        return o

    def barrier(self):
        lasts = []
        for e in ENGS:
            for o in reversed(self.ops[e]):
                if o.fn is not None:
                    lasts.append(o)
                    break
        for hist in self.dma_hist:
            if hist:
                lasts.append(hist[-1])
        for e in ENGS:
            o = self.op(e, None)
            for d in lasts:
                if d.eng == e and not d.dma and e == 'pe':
                    continue
                if d not in o.deps:
                    o.deps.append(d)
                if not d.dma:
                    d.sig = True
        self.last_w = {}
        self.readers = {}

    def emit(self):
        nc = self.nc
        cnt = {e: 0 for e in ENGS}
        for e in ENGS:
            for o in self.ops[e]:
                if o.sig and not o.dma:
                    assert o.fn is not None
                    cnt[e] += 1
                    o.cnt = cnt[e]
        self.sig_counts = cnt
        with ExitStack() as st:
            sems = {e: st.enter_context(nc.semaphore("s_" + e)) for e in ENGS}
            dsems = [st.enter_context(nc.semaphore("d_%d" % i)) for i in range(self.n_dma_sems)]
            block = st.enter_context(nc.Block())

            def run(engname, engobj):
                seen = {}
                for o in self.ops[engname]:
                    waits = {}
                    for d in o.deps:
                        if d.dma:
                            key = ('d', d.dsem)
                            val = d.dval
                        else:
                            key = ('e', d.eng)
                            val = d.cnt
                        if seen.get(key, 0) >= val:
                            continue
                        if waits.get(key, 0) < val:
                            waits[key] = val
                    if o.dma and o.prev_dma is not None:
                        key = ('d', o.dsem)
                        val = o.prev_dma.dval
                        if seen.get(key, 0) < val and waits.get(key, 0) < val:
                            waits[key] = val
                    for key, val in waits.items():
                        sem = dsems[key[1]] if key[0] == 'd' else sems[key[1]]
                        engobj.wait_ge(sem, val)
                        seen[key] = val
                    if o.fn is None:
                        continue
                    ins = o.fn(engobj)
                    if o.dma:
                        ins.then_inc(dsems[o.dsem], 16)
                    elif o.sig:
                        ins.then_inc(sems[o.eng], 1)

            @block.tensor
            def _(e):
                run('pe', e)

            @block.scalar
            def _(e):
                run('act', e)

            @block.vector
            def _(e):
                run('dve', e)

            @block.gpsimd
            def _(e):
                run('pool', e)

            @block.sync
            def _(e):
                run('sp', e)


class Arena:
    def __init__(self, ap, n):
        self.ap = ap
        self.n = n
        self.off = 0

    def take(self, n, parts=128):
        assert self.off + n <= self.n, ("arena overflow", self.off, n, self.n)
        a = self.ap[0:parts, self.off:self.off + n]
        self.off += n
        return a

    def reset(self, off=0):
        self.off = off


class K:
    def __init__(self, nc, P):
        self.nc = nc
        self.P = P

    def mm(self, out, pairs, reads, writes, first=True, last=True):
        pairs = list(pairs)

        def fn(e):
            n = len(pairs)
            ins = None
            for i, (l, r) in enumerate(pairs):
                ins = e.matmul(out, l, r, start=(first and i == 0), stop=(last and i == n - 1))
            return ins
        self.P.op('pe', fn, reads, writes)

    def tr(self, out, in_, reads, writes):
        ident = self.identb
        p = in_.shape[0]
        self.P.op('pe', lambda e: e.transpose(out, in_, ident[0:p, 0:p]), list(reads) + ['const'], writes)

    def act(self, out, in_, func, reads, writes, bias=None, scale=None, accum_out=None):
        kw = {}
        if bias is not None:
            kw['bias'] = bias
        if scale is not None:
            kw['scale'] = scale
        if accum_out is not None:
            kw['accum_out'] = accum_out
        self.P.op('act', lambda e: e.activation(out, in_, func, **kw), reads, writes)

    def tt(self, out, in0, in1, op, reads, writes, eng='dve'):
        self.P.op(eng, lambda e: e.tensor_tensor(out, in0, in1, op=op), reads, writes)

    def ts(self, out, in0, s1, s2, op0, op1, reads, writes, eng='dve'):
        if s2 is None:
            self.P.op(eng, lambda e: e.tensor_scalar(out, in0, s1, None, op0=op0), reads, writes)
        else:
            self.P.op(eng, lambda e: e.tensor_scalar(out, in0, s1, s2, op0=op0, op1=op1), reads, writes)

    def stt(self, out, in0, sc, in1, op0, op1, reads, writes, eng='dve'):
        self.P.op(eng, lambda e: e.scalar_tensor_tensor(out, in0, sc, in1, op0=op0, op1=op1), reads, writes)

    def cp(self, out, in_, reads, writes, eng='dve'):
        if eng == 'act':
            self.P.op('act', lambda e: e.copy(out, in_), reads, writes)
        else:
            self.P.op(eng, lambda e: e.tensor_copy(out, in_), reads, writes)

    def dma(self, out, in_, reads, writes, eng='sp'):
        self.P.op(eng, lambda e: e.dma_start(out=out, in_=in_), reads, writes, dma=True)


def build_program(nseq, layers, final_ln_only=False):
    nc = bass.Bass("TRN2", target_bir_lowering=False)
    P = Prog(nc)
    k = K(nc, P)
    dt_in = {}

    def din(name, shape, dtype=F32):
        t = nc.dram_tensor(name, list(shape), dtype, kind="ExternalInput").ap()
        dt_in[name] = t
        return t

    x_h = din("x", [nseq, S, D])
    pos_h = din("pos", [nseq, 1, S], I32)
    lng_h = din("ln_g", [DEPTH, 1, D])
    lnb_h = din("ln_b", [DEPTH, 1, D])
    cst_h = din("consts", [128, 7 * 128])
    gWA_h = din("gla_WA", [2, 4, 128, 8 * 768])
    gWZ_h = din("gla_WZ", [2, 4, 128, 8 * 512])
    gWO_h = din("gla_WO", [2, 4, 128, 4 * 1024])
    gWGL_h = din("gla_WGL", [2, 128, 8 * 32])
    gwg_h = din("gla_wg", [2, 4, 33, 256])
    ggn_h = din("gla_gn", [2, 128, 16])
    mWI_h = din("mla_WI", [2, 128, 8 * 896])
    mWH_h = din("mla_WH", [2, 16, 128, 3328])
    mqg_h = din("mla_qg", [2, 128, 3])
    mkg_h = din("mla_kg", [2, 128, 2])
    out_h = nc.dram_tensor("out", [nseq, S, D], F32, kind="ExternalOutput").ap()

    N16 = 47872
    N32 = 3712
    with ExitStack() as st:
        sb = lambda n, s, d: st.enter_context(nc.sbuf_tensor(n, s, d))
        x32 = sb("x32", [128, NT, D], F32)
        xT = sb("xT", [128, KC, S], BF16)
        cstb = sb("cstb", [128, 6 * 128], BF16)
        cstf = sb("cstf", [128, 3 * 128], F32)
        a16t = sb("a16", [128, N16], BF16)
        a32t = sb("a32", [128, N32], F32)
        ps = [st.enter_context(nc.psum_tensor("ps%d" % i, [128, 512], F32)) for i in range(7)]
        psT = st.enter_context(nc.psum_tensor("psT", [128, 1024], BF16))
        A16 = Arena(a16t, N16)
        A32 = Arena(a32t, N32)

        k.identb = cstb[:, 0:128]
        trif = cstb[:, 128:256]
        trib = cstb[:, 256:384]
        onesb = cstb[:, 640:768]
        maskfb = cstf[:, 0:256]
        invf = cstf[:, 256:257]
        shiftS = cstf[:, 257:258]

        k.dma(cstb[:, :], cst_h[:, 0:768], [], ['const'], eng='pool')
        k.dma(cstf[:, 0:256], cst_h[:, 384:640], [], ['const'])
        k.dma(cstf[:, 256:384], cst_h[:, 768:896], [], ['const'])

        psT_alt = [psT[:, :], ps[6][:, :].bitcast(BF16)]
        psT_names = ['psT', 'ps6']

        def make_xT_a(t):
            p = t % 2
            xb = k.xbs[p]
            pT = psT_alt[p]
            k.cp(xb, x32[:, t, :], ['x%d' % t], ['xb%d' % p], eng='act')
            for kc in range(KC):
                k.tr(pT[:, kc * 128:(kc + 1) * 128], xb[:, kc * 128:(kc + 1) * 128], ['xb%d' % p], [psT_names[p]])

        def make_xT_b(t):
            p = t % 2
            tokl = slice(t * 128, (t + 1) * 128)
            k.cp(xT[:, :, tokl], psT_alt[p].rearrange("p (k t) -> p k t", k=KC), [psT_names[p]], ['xT%d' % t])

        def make_xT(t):
            make_xT_a(t)
            make_xT_b(t)

        def layer_norm_phase(i, seq, last, nxt_layer=None):
            A32.reset()
            lng = A32.take(D)
            lnb = A32.take(D)
            sts = [A32.take(24) for _ in range(4)]
            k.xbs = [a16t[:, N16 - 2 * D:N16 - D], a16t[:, N16 - D:N16]]
            k.dma(lng, lng_h[i].partition_broadcast(128), [], ['lng'])
            k.dma(lnb, lnb_h[i].partition_broadcast(128), [], ['lnb'])
            if nxt_layer is not None:
                sv = A32.off
                prefetch(*nxt_layer)
                A32.reset(sv)

            junk = a16t[:, N16 - 3 * D:N16 - 2 * D]

            def s1(t):
                st6 = sts[t % 4]
                sn = 'st%d' % (t % 4)
                xr = 'x%d' % t
                xt_ = x32[:, t, :]
                P.op('act', lambda e: e.memzero(st6[:, 0:2]), [], [sn + 'a'])
                k.act(junk, xt_, AF.Identity, [xr, sn + 'a'], ['junk', sn + 'a'], accum_out=st6[:, 0:1])
                k.act(junk, xt_, AF.Square, [xr, sn + 'a'], ['junk', sn + 'b'], accum_out=st6[:, 1:2])
                k.ts(st6[:, 12:13], st6[:, 0:1], 1.0 / D, None, ALU.mult, None, [sn + 'a'], [sn + 'mv'])
                k.tt(st6[:, 2:3], st6[:, 12:13], st6[:, 12:13], ALU.mult, [sn + 'mv'], [sn + 'm2'])
                k.stt(st6[:, 13:14], st6[:, 1:2], 1.0 / D, st6[:, 2:3], ALU.mult, ALU.subtract, [sn + 'b', sn + 'm2'], [sn + 'var'])
                k.act(st6[:, 14:15], st6[:, 13:14], AF.Ln, [sn + 'var'], [sn + 'ln'], bias=EPS)
                k.act(st6[:, 15:16], st6[:, 14:15], AF.Exp, [sn + 'ln'], [sn + 'rs'], scale=-0.5)

            def s2(t):
                st6 = sts[t % 4]
                sn = 'st%d' % (t % 4)
                xr = 'x%d' % t
                xt_ = x32[:, t, :]
                k.stt(st6[:, 16:17], st6[:, 12:13], -1.0, st6[:, 15:16], ALU.mult, ALU.mult, [sn + 'mv', sn + 'rs'], [sn + 'nb'])
                k.act(xt_, xt_, AF.Identity, [xr, sn + 'rs', sn + 'nb'], [xr], bias=st6[:, 16:17], scale=st6[:, 15:16])

            def s3(t):
                xr = 'x%d' % t
                xt_ = x32[:, t, :]
                k.tt(xt_, xt_, lng, ALU.mult, [xr, 'lng'], [xr])
                k.tt(xt_, xt_, lnb, ALU.add, [xr, 'lnb'], [xr])

            def s4(t):
                xr = 'x%d' % t
                if last:
                    k.dma(out_h[seq, t * 128:(t + 1) * 128, :], x32[:, t, :], [xr], ['out%d' % t])
                else:
                    make_xT_a(t)

            for n in range(NT + 4):
                if n < NT:
                    s1(n)
                if 0 <= n - 1 < NT:
                    s2(n - 1)
                if 0 <= n - 2 < NT:
                    s3(n - 2)
                if 0 <= n - 3 < NT:
                    s4(n - 3)
                if 0 <= n - 4 < NT and not last:
                    make_xT_b(n - 4)

        def gla_layer(j, i, seq):
            A16.reset()
            A32.reset()
            WA = A16.take(8 * 768).rearrange("p (k f) -> p k f", k=8)
            WZ = A16.take(8 * 512).rearrange("p (k f) -> p k f", k=8)
            WO = A16.take(4 * 1024).rearrange("p (k f) -> p k f", k=4)
            qtf = A16.take(S)
            qtb = A16.take(S)
            ktf = A16.take(S)
            ktb = A16.take(S)
            ktok = A16.take(NT * 128).rearrange("p (t f) -> p t f", t=NT)
            v = A16.take(NT * 512).rearrange("p (t f) -> p t f", t=NT)
            snap = A16.take(NT * 512).rearrange("p (t f) -> p t f", t=NT)
            glrT = A16.take(S)
            wg = A16.take(256)
            WGL = A16.take(8 * 32).rearrange("p (k f) -> p k f", k=8)
            sp2 = A16.take(512)
            sT = [A16.take(256), A16.take(256)]
            sbf = [A16.take(512), A16.take(512)]
            ktmp = [A16.take(128), A16.take(128)]
            og = [A16.take(512), A16.take(512)]
            ogT = [A16.take(512).rearrange("p (c t) -> p c t", c=4), A16.take(512).rearrange("p (c t) -> p c t", c=4)]
            etmp = A32.take(512)
            eq = [A32.take(256), A32.take(256)]
            ek = [A32.take(256), A32.take(256)]
            Rf = A32.take(512)
            Rb = A32.take(512)
            ebf = A32.take(16)
            ebb = A32.take(16)
            sg = A32.take(512)
            zs = A32.take(512)
            sm = A32.take(8)
            gn = A32.take(16)
            lnscale = math.log(128 ** -0.5)
            vbank = [2, 6]

            if k.prefetch_only:
                k.dma(WGL, gWGL_h[j].rearrange("p (k f) -> p k f", k=8), [], ['WGL'], eng='pool')
                k.dma(gn, ggn_h[j], [], ['gn'])
                k.dma(WA, gWA_h[j, 0].rearrange("p (k f) -> p k f", k=8), [], ['WA'], eng='pool')
                k.dma(wg[0:33, :], gwg_h[j, 0], [], ['wg'], eng='pool')
                k.dma(WZ, gWZ_h[j, 0].rearrange("p (k f) -> p k f", k=8), [], ['WZ'], eng='pool')
                k.dma(WO, gWO_h[j, 0].rearrange("p (k f) -> p k f", k=4), [], ['WO'], eng='pool')
                return
            P.op('dve', lambda e: e.memset(glrT[32:33, :], 1.0), [], ['glrT'])
            for g in range(4):
                tg = slice(g * 512, (g + 1) * 512)
                k.mm(ps[0][0:32, :], [(WGL[:, kc, :], xT[:, kc, tg]) for kc in range(KC)],
                     ['WGL', 'xT'], ['ps0'])
                k.cp(glrT[0:32, tg], ps[0][0:32, :], ['ps0'], ['glrT'])

            for h in range(4):
                if h > 0:
                    k.dma(wg[0:33, :], gwg_h[j, h], [], ['wg'], eng='pool')
                    k.dma(WZ, gWZ_h[j, h].rearrange("p (k f) -> p k f", k=8), [], ['WZ'], eng='pool')
                    k.dma(WO, gWO_h[j, h].rearrange("p (k f) -> p k f", k=4), [], ['WO'], eng='pool')

                for c in range(4):
                    k.ts(WO[:, c, :], WO[:, c, :], gn[:, 4 * h + c:4 * h + c + 1], None, ALU.mult, None, ['WO', 'gn'], ['WO'])

                def gate(pr):
                    for u in (0, 1):
                        tl = slice((2 * pr + u) * 128, (2 * pr + u + 1) * 128)
                        k.mm(ps[3][:, u * 256:(u + 1) * 256], [(glrT[0:33, tl], wg[0:33, :])], ['glrT', 'wg'], ['ps3'])
                    k.act(etmp, ps[3][:, :], AF.Exp, ['ps3'], ['etmp'], scale=-1.0)
                    k.act(sp2, etmp, AF.Ln, ['etmp'], ['sp2'], bias=1.0)

                def bw_state(t):
                    k.mm(ps[5][:, :], [(ktmp[t % 2], v[:, t, :])], ['ktmp%d' % (t % 2), 'v%d' % t], ['ps5'])
                    if t == NT - 1:
                        k.cp(Rb, ps[5][:, :], ['ps5'], ['Rb'])
                    else:
                        k.stt(Rb, Rb, ebb[:, t + 1:t + 2], ps[5][:, :], ALU.mult, ALU.add,
                              ['Rb', 'ebb%d' % (t + 1), 'ps5'], ['Rb'])
                    k.act(snap[:, t - 1, :], Rb, AF.Identity, ['Rb', 'ebb%d' % t], ['snap%d' % (t - 1)],
                          scale=ebb[:, t:t + 1])

                gate(7)
                for pr in reversed(range(8)):
                    ta, tb = 2 * pr + 1, 2 * pr
                    pq = ps[pr % 2]
                    pqn = 'ps%d' % (pr % 2)
                    tp2 = slice(pr * 256, (pr + 1) * 256)
                    k.mm(pq[:, 0:256], [(WA[:, kc, 0:128], xT[:, kc, tp2]) for kc in range(KC)], ['WA', 'xT'], [pqn])
                    k.mm(pq[:, 256:512], [(WA[:, kc, 128:256], xT[:, kc, tp2]) for kc in range(KC)], ['WA', 'xT'], [pqn])
                    if ta + 2 <= NT - 1:
                        bw_state(ta + 2)
                    for t in (ta, tb):
                        u = t - tb
                        par = t % 2
                        tl = slice(t * 128, (t + 1) * 128)
                        pb = ps[4][:, par * 256:(par + 1) * 256]
                        pbn = 'ps4_%d' % par
                        k.mm(pb[:, 0:128], [(sp2[:, u * 256:u * 256 + 128], trif)], ['sp2', 'const'], [pbn])
                        k.mm(pb[:, 128:256], [(sp2[:, u * 256 + 128:u * 256 + 256], trib)], ['sp2', 'const'], [pbn])
                        k.act(eq[par], pb, AF.Exp, [pbn], ['eq%d' % par], bias=lnscale)
                        k.act(ek[par], pb, AF.Exp, [pbn], ['ek%d' % par], scale=-1.0)
                        k.act(ebf[:, t:t + 1], pb[:, 127:128], AF.Exp, [pbn], ['ebf%d' % t])
                        k.act(ebb[:, t:t + 1], pb[:, 128:129], AF.Exp, [pbn], ['ebb%d' % t])
                        c0 = u * 128
                        k.tt(qtf[:, tl], pq[:, c0:c0 + 128], eq[par][:, 0:128], ALU.mult, [pqn, 'eq%d' % par], ['q%d' % t])
                        k.tt(qtb[:, tl], pq[:, c0:c0 + 128], eq[par][:, 128:256], ALU.mult, [pqn, 'eq%d' % par], ['q%d' % t])
                        k.tt(ktf[:, tl], pq[:, 256 + c0:256 + c0 + 128], ek[par][:, 0:128], ALU.mult, [pqn, 'ek%d' % par], ['k%d' % t])
                        k.tt(ktb[:, tl], pq[:, 256 + c0:256 + c0 + 128], ek[par][:, 128:256], ALU.mult, [pqn, 'ek%d' % par], ['k%d' % t])
                    if pr > 0:
                        gate(pr - 1)
                    for t in (ta, tb):
                        tl = slice(t * 128, (t + 1) * 128)
                        vb = vbank[t % 2]
                        k.mm(ps[vb][:, :], [(xT[:, kc, tl], WA[:, kc, 256:768]) for kc in range(KC)], ['WA', 'xT'], ['ps%d' % vb])
                        k.cp(v[:, t, :], ps[vb][:, :], ['ps%d' % vb], ['v%d' % t], eng='act')
                        if t == ta and tb + 2 <= NT - 1:
                            bw_state(tb + 2)
                    for t in (ta, tb):
                        tl = slice(t * 128, (t + 1) * 128)
                        par = t % 2
                        pc = psT[:, par * 256:(par + 1) * 256]
                        pcn = 'psT_%d' % par
                        k.tr(pc[:, 0:128], ktf[:, tl], ['k%d' % t], [pcn])
                        k.tr(pc[:, 128:256], ktb[:, tl], ['k%d' % t], [pcn])
                        k.cp(ktok[:, t, :], pc[:, 0:128], [pcn], ['ktok%d' % t])
                        k.cp(ktmp[par], pc[:, 128:256], [pcn], ['ktmp%d' % par])
                bw_state(1)
                if h < 3:
                    k.dma(WA, gWA_h[j, h + 1].rearrange("p (k f) -> p k f", k=8), [], ['WA'], eng='pool')

                def stage_a_pe(t):
                    tl = slice(t * 128, (t + 1) * 128)
                    par = t % 2
                    po = ps[3] if par == 0 else ps[6]
                    pon = 'ps3' if par == 0 else 'ps6'
                    k.mm(ps[0][:, 0:128], [(ktf[:, tl], qtf[:, tl])], ['k%d' % t, 'q%d' % t], ['ps0'])
                    k.mm(ps[0][:, 128:256], [(ktb[:, tl], qtb[:, tl])], ['k%d' % t, 'q%d' % t], ['ps0'])
                    k.tt(sT[par], ps[0][:, 0:256], maskfb, ALU.mult, ['ps0', 'const'], ['sT%d' % par])
                    if t < NT - 1:
                        k.mm(ps[1][:, :], [(ktok[:, t, :], v[:, t, :])], ['ktok%d' % t, 'v%d' % t], ['ps1'])
                        if t == 0:
                            k.cp(Rf, ps[1][:, :], ['ps1'], ['Rf'])
                        else:
                            k.stt(Rf, Rf, ebf[:, t - 1:t], ps[1][:, :], ALU.mult, ALU.add,
                                  ['Rf', 'ebf%d' % (t - 1), 'ps1'], ['Rf'])
                        k.act(sbf[par], Rf, AF.Identity, ['Rf', 'ebf%d' % t], ['sbf%d' % par], scale=ebf[:, t:t + 1])
                    k.mm(ps[2][:, :], [(xT[:, kc, tl], WZ[:, kc, :]) for kc in range(KC)], ['WZ', 'xT'], ['ps2'])
                    k.cp(etmp, ps[2][:, :], ['ps2'], ['etmp'])
                    k.act(sg, etmp, AF.Exp, ['etmp'], ['sg'], scale=-1.0)
                    k.act(sg, sg, AF.Ln, ['sg'], ['sg'], bias=1.0)
                    k.act(sg, sg, AF.Exp, ['sg'], ['sg'], scale=-1.0)
                    pairs = []
                    rd = ['sT%d' % par, 'v%d' % t, 'q%d' % t]
                    if t > 0:
                        pairs.append((qtf[:, tl], sbf[(t - 1) % 2]))
                        rd.append('sbf%d' % ((t - 1) % 2))
                    pairs.append((sT[par][:, 0:128], v[:, t, :]))
                    if t < NT - 1:
                        pairs.append((qtb[:, tl], snap[:, t, :]))
                        rd.append('snap%d' % t)
                    pairs.append((sT[par][:, 128:256], v[:, t, :]))
                    k.mm(po[:, :], pairs, rd, [pon])
                    P.op('act', lambda e: e.memzero(sm[:, 0:1]), [], ['ss'])
                    k.act(og[par], po[:, :], AF.Square, [pon], ['og%d' % par, 'ss'], accum_out=sm[:, 0:1])
                    k.act(sm[:, 1:2], sm[:, 0:1], AF.Ln, ['ss'], ['lnv'], bias=EPS, scale=1.0 / 512)
                    k.act(sm[:, 2 + par:3 + par], sm[:, 1:2], AF.Exp, ['lnv'], ['rstd%d' % par], scale=-0.5)

                def stage_a_tail(t):
                    par = t % 2
                    po = ps[3] if par == 0 else ps[6]
                    pon = 'ps3' if par == 0 else 'ps6'
                    k.tt(zs, etmp, sg, ALU.mult, ['etmp', 'sg'], ['zs'])
                    k.stt(og[par], po[:, :], sm[:, 2 + par:3 + par], zs, ALU.mult, ALU.mult, [pon, 'rstd%d' % par, 'zs'], ['og%d' % par])

                def stage_b1(t):
                    par = t % 2
                    for c in range(4):
                        k.tr(psT[:, 512 + c * 128:512 + (c + 1) * 128], og[par][:, c * 128:(c + 1) * 128], ['og%d' % par], ['psTb'])
                    k.cp(ogT[par], psT[:, 512:1024].rearrange("p (c t) -> p c t", c=4), ['psTb'], ['ogT%d' % par])

                def stage_b2(t):
                    par = t % 2
                    xr = 'x%d' % t
                    k.mm(ps[4][:, :], [(ogT[par][:, c, :], WO[:, c, 0:512]) for c in range(4)], ['ogT%d' % par, 'WO'], ['ps4_0', 'ps4_1'])
                    k.mm(ps[5][:, :], [(ogT[par][:, c, :], WO[:, c, 512:1024]) for c in range(4)], ['ogT%d' % par, 'WO'], ['ps5'])
                    for half, pb in ((0, 4), (1, 5)):
                        xs = x32[:, t, half * 512:(half + 1) * 512]
                        pbn = ['ps4_0', 'ps4_1'] if pb == 4 else ['ps5']
                        if h == 0:
                            k.stt(xs, xs, ALPHA, ps[pb][:, :], ALU.mult, ALU.add, [xr] + pbn, [xr])
                        else:
                            k.tt(xs, xs, ps[pb][:, :], ALU.add, [xr] + pbn, [xr])

                stage_a_pe(0)
                stage_a_tail(0)
                for t in range(NT):
                    if t + 1 < NT:
                        stage_a_pe(t + 1)
                    stage_b1(t)
                    if t >= 1:
                        stage_b2(t - 1)
                    if t + 1 < NT:
                        stage_a_tail(t + 1)
                stage_b2(NT - 1)

        def mla_layer(j, i, seq):
            A16.reset()
            A32.reset()
            WI = A16.take(8 * 896).rearrange("p (k f) -> p k f", k=8)
            cqn = A16.take(3 * S).rearrange("p (k t) -> p k t", k=3)
            ckvn = A16.take(2 * S).rearrange("p (k t) -> p k t", k=2)
            Kpp = A16.take(S)
            CC = A16.take(S)
            SS = A16.take(S)
            qn = A16.take(S)
            qr = A16.take(S)
            kn = A16.take(S)
            vv = A16.take(NT * 128)
            silu = A16.take(S)
            ogT = A16.take(S)
            WHb = [A16.take(3328), A16.take(3328)]
            PT = [A16.take(512), A16.take(512), A16.take(512)]
            sq = [A16.take(512), A16.take(512), A16.take(512)]
            sq3 = A16.take(512)
            f0 = A32.take(512)
            f1 = A32.take(512)
            f2 = A32.take(512)
            f3 = A32.take(512)
            rstd = A32.take(512)
            rec = A32.take(512)
            qg = A32.take(3)
            kg = A32.take(2)
            sc = 192 ** -0.5
            TWO_PI = 2.0 * math.pi

            if k.prefetch_only:
                k.dma(WI, mWI_h[j].rearrange("p (k f) -> p k f", k=8), [], ['WI'], eng='pool')
                k.dma(qg, mqg_h[j], [], ['qg'])
                k.dma(kg, mkg_h[j], [], ['kg'])
                k.dma(WHb[0], mWH_h[j, 0], [], ['WH0'], eng='pool')
                return
            f0i = f0.bitcast(I32)
            f3i = f3.bitcast(I32)
            for g in range(4):
                tg = slice(g * 512, (g + 1) * 512)
                for (c0, nch, dst, gcol, gname, width) in ((0, 3, cqn, qg, 'qg', 384.0), (384, 2, ckvn, kg, 'kg', 256.0)):
                    for c in range(nch):
                        k.mm(ps[c][:, :], [(WI[:, kc, c0 + c * 128:c0 + (c + 1) * 128], xT[:, kc, tg]) for kc in range(KC)],
                             ['WI', 'xT'], ['ps%d' % c])
                    for c in range(nch):
                        k.act(sq[c], ps[c][:, :], AF.Square, ['ps%d' % c], ['sq%d' % c])
                        k.mm(ps[3][:, :], [(onesb, sq[c])], ['sq%d' % c, 'const'], ['ps3'], first=(c == 0), last=(c == nch - 1))
                    k.act(rstd, ps[3][:, :], AF.Ln, ['ps3'], ['rstd'], bias=EPS, scale=1.0 / width)
                    k.act(rstd, rstd, AF.Exp, ['rstd'], ['rstd'], scale=-0.5)
                    for c in range(nch):
                        k.stt(dst[:, c, tg], ps[c][:, :], gcol[:, c:c + 1], rstd, ALU.mult, ALU.mult,
                              ['ps%d' % c, gname, 'rstd'], ['lowrank'])
                k.mm(ps[4][:, :], [(WI[:, kc, 640:768], xT[:, kc, tg]) for kc in range(KC)], ['WI', 'xT'], ['ps4'])
                k.mm(ps[5][:, :], [(WI[:, kc, 768:896], xT[:, kc, tg]) for kc in range(KC)], ['WI', 'xT'], ['ps5'])
                k.dma(f0i, pos_h[seq, 0:1, tg].partition_broadcast(128), [], ['f0'])
                k.cp(f1, f0i, ['f0'], ['f1'])
                k.ts(f1, f1, invf, 1.0 / TWO_PI, ALU.mult, ALU.mult, ['f1', 'const'], ['f1'])
                for tab, shift in ((CC, 0.25), (SS, shiftS)):
                    k.ts(f2, f1, shift, None, ALU.add, None, ['f1', 'const'], ['f2'])
                    k.cp(f3i, f2, ['f2'], ['f3'])
                    k.cp(f0, f3i, ['f3'], ['f0'])
                    k.tt(f2, f2, f0, ALU.subtract, ['f2', 'f0'], ['f2'])
                    k.ts(f0, f2, 0.5, None, ALU.is_gt, None, ['f2'], ['f0'])
                    k.tt(f2, f2, f0, ALU.subtract, ['f2', 'f0'], ['f2'])
                    k.ts(f0, f2, -0.5, None, ALU.is_lt, None, ['f2'], ['f0'])
                    k.tt(f2, f2, f0, ALU.add, ['f2', 'f0'], ['f2'])
                    k.act(tab[:, tg], f2, AF.Sin, ['f2'], ['tab'], scale=TWO_PI)
                k.tt(f0, ps[4][:, :], CC[:, tg], ALU.mult, ['ps4', 'tab'], ['f0'])
                k.tt(f1, ps[5][:, :], SS[:, tg], ALU.mult, ['ps5', 'tab'], ['f1'])
                k.tt(Kpp[:, tg], f0, f1, ALU.add, ['f0', 'f1'], ['Kpp'])

            def proj_chunks(h):
                WH = WHb[h % 2]
                whn = 'WH%d' % (h % 2)
                wuq = WH[:, 0:768].rearrange("p (k f) -> p k f", k=3)
                wukv = WH[:, 768:1280].rearrange("p (k f) -> p k f", k=2)
                wz = WH[:, 1280:2304].rearrange("p (k f) -> p k f", k=8)
                out = []
                for g in range(4):
                    tg = slice(g * 512, (g + 1) * 512)

                    def c_qn(tg=tg):
                        k.mm(ps[0][:, :], [(wuq[:, kc, 0:128], cqn[:, kc, tg]) for kc in range(3)], [whn, 'lowrank'], ['ps0'])
                        k.cp(qn[:, tg], ps[0][:, :], ['ps0'], ['qn'], eng='act')

                    def c_qr(tg=tg):
                        k.mm(ps[1][:, :], [(wuq[:, kc, 128:256], cqn[:, kc, tg]) for kc in range(3)], [whn, 'lowrank'], ['ps1'])
                        k.tt(qr[0:64, tg], ps[1][0:64, :], CC[0:64, tg], ALU.mult, ['ps1', 'tab'], ['qr'])
                        k.tt(qr[64:128, tg], ps[1][64:128, :], SS[64:128, tg], ALU.mult, ['ps1', 'tab'], ['qr'])

                    def c_kn(tg=tg):
                        k.mm(ps[0][:, :], [(wukv[:, kc, 0:128], ckvn[:, kc, tg]) for kc in range(2)], [whn, 'lowrank'], ['ps0'])
                        k.cp(kn[:, tg], ps[0][:, :], ['ps0'], ['kn'], eng='act')

                    def c_z(tg=tg):
                        k.mm(ps[2][:, :], [(wz[:, kc, :], xT[:, kc, tg]) for kc in range(KC)], [whn, 'xT'], ['ps2'])
                        k.act(f0, ps[2][:, :], AF.Exp, ['ps2'], ['f0'], scale=-1.0)
                        k.act(f0, f0, AF.Ln, ['f0'], ['f0'], bias=1.0)
                        k.act(f0, f0, AF.Exp, ['f0'], ['f0'], scale=-1.0)
                        k.tt(silu[:, tg], ps[2][:, :], f0, ALU.mult, ['ps2', 'f0'], ['silu'])

                    def c_v(g=g):
                        for u in range(4):
                            tl = slice((4 * g + u) * 128, (4 * g + u + 1) * 128)
                            k.mm(ps[1][:, u * 128:(u + 1) * 128], [(ckvn[:, kc, tl], wukv[:, kc, 128:256]) for kc in range(2)],
                                 [whn, 'lowrank'], ['ps1'])
                        k.cp(vv[:, g * 512:(g + 1) * 512], ps[1][:, :], ['ps1'], ['vv'])

                    out += [c_qn, c_qr, c_z, c_kn, c_v]
                return out

            for c in proj_chunks(0):
                c()
            for h in range(16):
                WH = WHb[h % 2]
                whn = 'WH%d' % (h % 2)
                if h < 15:
                    k.dma(WHb[(h + 1) % 2], mWH_h[j, h + 1], [], ['WH%d' % ((h + 1) % 2)], eng='pool')
                wo = WH[:, 2304:3328]
                blocks = [(g, kt) for g in range(4) for kt in range(NT)]
                Sb = [5, 6, 0]
                accD = [rstd, f2]
                accDn = ['rstd', 'f2']

                def emit_S(bi):
                    g, kt = blocks[bi]
                    qs = slice(g * 512, (g + 1) * 512)
                    ktl = slice(kt * 128, (kt + 1) * 128)
                    b = bi % 3
                    pS = ps[Sb[b]]
                    k.mm(pS[:, :], [(kn[:, ktl], qn[:, qs]), (Kpp[:, ktl], qr[:, qs])], ['kn', 'qn', 'Kpp', 'qr'], ['ps%d' % Sb[b]])
                    k.act(PT[b], pS[:, :], AF.Exp, ['ps%d' % Sb[b]], ['PT%d' % b], scale=sc)

                def finish(g):
                    qs = slice(g * 512, (g + 1) * 512)
                    pO, pD = (3, 4) if g % 2 == 0 else (1, 2)
                    k.mm(ps[pD][:, :], [(onesb, sq[0]), (onesb, sq[1])], ['const', 'sq0', 'sq1'], ['ps%d' % pD], first=False, last=True)
                    k.act(rec, ps[pD][:, :], AF.Ln, ['ps%d' % pD], ['rec'])
                    k.act(rec, rec, AF.Exp, ['rec'], ['rec'], scale=-1.0)
                    k.tt(f1, ps[pO][:, :], rec, ALU.mult, ['ps%d' % pO, 'rec'], ['f1'])
                    k.tt(ogT[:, qs], f1, silu[:, qs], ALU.mult, ['f1', 'silu'], ['ogT%d' % g])

                emit_S(0)
                emit_S(1)
                pending = []
                for bi, (g, kt) in enumerate(blocks):
                    if bi + 2 < len(blocks):
                        emit_S(bi + 2)
                    ktl = slice(kt * 128, (kt + 1) * 128)
                    b = bi % 3
                    pO, pD = (3, 4) if g % 2 == 0 else (1, 2)
                    k.mm(ps[pO][:, :], [(vv[:, ktl], PT[b])], ['vv', 'PT%d' % b], ['ps%d' % pO], first=(kt == 0), last=(kt == NT - 1))
                    if kt % 2 == 0:
                        k.mm(ps[pD][:, :], [(onesb, PT[b])], ['const', 'PT%d' % b], ['ps%d' % pD], first=(kt == 0), last=False)
                    else:
                        acc, an = accD[g % 2], accDn[g % 2]
                        if kt == 1:
                            k.cp(acc, PT[b], ['PT%d' % b], [an])
                        else:
                            k.tt(acc, acc, PT[b], ALU.add, [an, 'PT%d' % b], [an])
                    if pending and pending[0][1] == bi:
                        finish(pending.pop(0)[0])
                    if kt == NT - 1:
                        k.cp(sq[0], accD[g % 2], [accDn[g % 2]], ['sq0'])
                        k.tt(sq[1], accD[g % 2], sq[0], ALU.subtract, [accDn[g % 2], 'sq0'], ['sq1'])
                        pending.append((g, bi + 4))
                while pending:
                    finish(pending.pop(0)[0])
                nxt = proj_chunks(h + 1) if h < 15 else []
                for t in range(NT):
                    tl = slice(t * 128, (t + 1) * 128)
                    xr = 'x%d' % t
                    for half in (0, 1):
                        pb = (5 + half) if t % 2 == 0 else (3 + half)
                        k.mm(ps[pb][:, :], [(ogT[:, tl], wo[:, half * 512:(half + 1) * 512])], ['ogT%d' % (t // 4), whn], ['ps%d' % pb])
                        xs = x32[:, t, half * 512:(half + 1) * 512]
                        xrh = xr + '_%d' % half
                        if h == 0:
                            k.stt(xs, xs, ALPHA, ps[pb][:, :], ALU.mult, ALU.add, [xrh, 'ps%d' % pb], [xrh])
                        else:
                            k.tt(xs, xs, ps[pb][:, :], ALU.add, [xrh, 'ps%d' % pb], [xrh])
                    if nxt:
                        nxt.pop(0)()
                    if nxt and t % 4 == 3:
                        nxt.pop(0)()
                while nxt:
                    nxt.pop(0)()

        def prefetch(kind, j):
            k.prefetch_only = True
            if kind == 'gla':
                gla_layer(j, None, None)
            else:
                mla_layer(j, None, None)
            k.prefetch_only = False

        for seq in range(nseq):
            P.barrier()
            k.xbs = [a16t[:, N16 - 2 * D:N16 - D], a16t[:, N16 - D:N16]]
            prefetch(*layers[0][:2])
            for t in range(NT):
                k.dma(x32[:, t, :], x_h[seq, t * 128:(t + 1) * 128, :], [], ['x%d' % t])
                make_xT(t)
            for li, (kind, j, i) in enumerate(layers):
                P.barrier()
                k.prefetch_only = False
                if kind == 'gla':
                    gla_layer(j, i, seq)
                else:
                    mla_layer(j, i, seq)
                P.barrier()
                nxt_layer = layers[li + 1][:2] if li + 1 < len(layers) else None
                layer_norm_phase(i, seq, last=(li == len(layers) - 1), nxt_layer=nxt_layer)
        P.barrier()
        P.emit()
    return nc


def host_consts():
    c = np.zeros((128, 7 * 128), np.float32)
    c[:, 0:128] = np.eye(128)
    jj = np.arange(128)[:, None]
    ii = np.arange(128)[None, :]
    c[:, 128:256] = np.where(jj <= ii, -1.0 / 16, 0.0)
    c[:, 256:384] = np.where(jj >= ii, -1.0 / 16, 0.0)
    c[:, 384:512] = (jj <= ii)
    c[:, 512:640] = (jj >= ii)
    c[:, 640:768] = 1.0
    inv_freq = (1.0 / (10000.0 ** (np.arange(0, 64, 2, dtype=np.float32) / 64))).astype(np.float32)
    c[:, 768] = inv_freq[np.arange(128) % 32]
    c[:, 769] = np.where((np.arange(128) // 32) % 2 == 0, 0.5, 0.0)
    return c


def host_layout(inp):
    f = np.float32
    d = {}
    d["ln_g"] = np.ascontiguousarray(inp["ln_g"], f).reshape(DEPTH, 1, D)
    d["ln_b"] = np.ascontiguousarray(inp["ln_b"], f).reshape(DEPTH, 1, D)
    d["consts"] = host_consts()
    w_in = np.asarray(inp["gla_w_in"], f)
    WA = np.zeros((2, 4, 128, 8, 768), f)
    WZ = np.zeros((2, 4, 128, 8, 512), f)
    for h in range(4):
        blk = np.concatenate([w_in[:, :, h * 128:(h + 1) * 128], w_in[:, :, 512 + h * 128:512 + (h + 1) * 128],
                              w_in[:, :, 1024 + h * 512:1024 + (h + 1) * 512]], axis=2)
        WA[:, h] = blk.reshape(2, 8, 128, 768).transpose(0, 2, 1, 3)
        zb = w_in[:, :, 3072 + h * 512:3072 + (h + 1) * 512]
        WZ[:, h] = zb.reshape(2, 8, 128, 512).transpose(0, 2, 1, 3)
    d["gla_WA"] = WA.reshape(2, 4, 128, 8 * 768)
    d["gla_WZ"] = WZ.reshape(2, 4, 128, 8 * 512)
    wo = np.asarray(inp["gla_w_out"], f)
    d["gla_WO"] = np.ascontiguousarray(wo.reshape(2, 4, 4, 128, 1024).transpose(0, 1, 3, 2, 4)).reshape(2, 4, 128, 4096)
    d["gla_WGL"] = np.ascontiguousarray(w_in[:, :, 5120:5152].reshape(2, 8, 128, 32).transpose(0, 2, 1, 3)).reshape(2, 128, 256)
    wgate = np.asarray(inp["gla_w_gate"], f)
    bgate = np.asarray(inp["gla_b_gate"], f)
    wg = np.zeros((2, 4, 33, 256), f)
    for h in range(4):
        wg[:, h, 0:16, 0:128] = wgate[:, 0, :, h * 128:(h + 1) * 128]
        wg[:, h, 16:32, 128:256] = wgate[:, 1, :, h * 128:(h + 1) * 128]
        wg[:, h, 32, 0:128] = bgate[:, 0, h * 128:(h + 1) * 128]
        wg[:, h, 32, 128:256] = bgate[:, 1, h * 128:(h + 1) * 128]
    d["gla_wg"] = wg
    d["gla_gn"] = np.ascontiguousarray(np.asarray(inp["gla_gn_g"], f).reshape(2, 16, 128).transpose(0, 2, 1))
    mw = np.asarray(inp["mla_w_in"], f)
    kr = mw[:, :, 640:704]
    krP = np.concatenate([kr[:, :, 32:64], kr[:, :, 0:32]], axis=2)
    wi = np.concatenate([mw[:, :, 0:640], kr, kr, krP, krP], axis=2)
    d["mla_WI"] = np.ascontiguousarray(wi.reshape(2, 8, 128, 896).transpose(0, 2, 1, 3)).reshape(2, 128, 8 * 896)
    uq = np.asarray(inp["mla_w_uq"], f).reshape(2, 3, 128, 16, 192)
    ukv = np.asarray(inp["mla_w_ukv"], f).reshape(2, 2, 128, 16, 256)
    mz = mw[:, :, 704:2752].reshape(2, 8, 128, 16, 128)
    mo = np.asarray(inp["mla_w_out"], f).reshape(2, 16, 128, 1024)
    WH = np.zeros((2, 16, 128, 3328), f)
    for h in range(16):
        qn = uq[:, :, :, h, 0:128]
        qr = uq[:, :, :, h, 128:192]
        qrP = np.concatenate([qr[..., 32:64], qr[..., 0:32]], axis=-1)
        blkq = np.concatenate([qn, qr, qrP], axis=-1)
        WH[:, h, :, 0:768] = blkq.transpose(0, 2, 1, 3).reshape(2, 128, 768)
        WH[:, h, :, 768:1280] = ukv[:, :, :, h, :].transpose(0, 2, 1, 3).reshape(2, 128, 512)
        WH[:, h, :, 1280:2304] = mz[:, :, :, h, :].transpose(0, 2, 1, 3).reshape(2, 128, 1024)
        WH[:, h, :, 2304:3328] = mo[:, h]
    d["mla_WH"] = WH
    d["mla_qg"] = np.ascontiguousarray(np.asarray(inp["mla_q_norm_g"], f).reshape(2, 3, 128).transpose(0, 2, 1))
    d["mla_kg"] = np.ascontiguousarray(np.asarray(inp["mla_kv_norm_g"], f).reshape(2, 2, 128).transpose(0, 2, 1))
    return d


ALL_LAYERS = [('gla', 0, 0), ('mla', 0, 1), ('gla', 1, 2), ('mla', 1, 3)]


FUSED = True


def _run(x, pos, shared, layers):
    nc = build_program(NSEQ, layers)
    in_maps = []
    for c in range(NCORES):
        m = dict(shared)
        m["x"] = np.ascontiguousarray(x[c * NSEQ:(c + 1) * NSEQ])
        m["pos"] = np.ascontiguousarray(pos[c * NSEQ:(c + 1) * NSEQ]).reshape(NSEQ, 1, S)
        in_maps.append(m)
    res = run_bass_kernel_spmd(nc, in_maps, core_ids=list(range(NCORES)))
    return np.concatenate([np.asarray(r["out"], np.float32) for r in res.results], axis=0)


def kernel(**inputs):
    x = np.asarray(inputs["x"], np.float32)
    pos = np.asarray(inputs["positions"], np.int32)
    shared = host_layout(inputs)
    if FUSED:
        return _run(x, pos, shared, ALL_LAYERS)
    for lay in ALL_LAYERS:
        x = _run(x, pos, shared, [lay])
    return x
```

```python
import math
from contextlib import ExitStack

import numpy as np
import concourse.bass as bass
import concourse.mybir as mybir
from concourse.bass_utils import run_bass_kernel_spmd

F32 = mybir.dt.float32
BF16 = mybir.dt.bfloat16
I32 = mybir.dt.int32
AF = mybir.ActivationFunctionType
ALU = mybir.AluOpType

ENGS = ('pe', 'act', 'dve', 'pool', 'sp')

S = 2048
D = 1024
NT = 16
KC = 8
DEPTH = 4
ALPHA = (2 * DEPTH) ** 0.25
EPS = 1e-5
NSEQ = 4
NCORES = 8


class Op:
    __slots__ = ('eng', 'fn', 'deps', 'sig', 'cnt', 'dma', 'dsem', 'dval', 'prev_dma')


class Prog:
    def __init__(self, nc, n_dma_sems=16, same_engine_sync=True):
        self.nc = nc
        self.ops = {e: [] for e in ENGS}
        self.last_w = {}
        self.readers = {}
        self.n_dma_sems = n_dma_sems
        self.dma_count = 0
        self.dma_hist = [[] for _ in range(n_dma_sems)]
        self.same_engine_sync = same_engine_sync

    def op(self, eng, fn, reads=(), writes=(), dma=False):
        o = Op()
        o.eng = eng
        o.fn = fn
        o.sig = False
        o.cnt = None
        o.dma = dma
        o.dsem = None
        o.dval = None
        o.prev_dma = None
        deps = []
        seen = set()

        def add(d, raw):
            if d is None or id(d) in seen:
                return
            if (not d.dma) and (not dma) and d.eng == eng:
                if eng == 'pe' or not self.same_engine_sync or not raw:
                    return
            seen.add(id(d))
            deps.append(d)

        for r in reads:
            add(self.last_w.get(r), True)
        for w in writes:
            add(self.last_w.get(w), False)
            for rd in self.readers.get(w, ()):
                add(rd, False)
        o.deps = deps
        for d in deps:
            if not d.dma:
                d.sig = True
        if dma:
            k = self.dma_count % self.n_dma_sems
            self.dma_count += 1
            hist = self.dma_hist[k]
            o.prev_dma = hist[-1] if hist else None
            hist.append(o)
            o.dsem = k
            o.dval = 16 * len(hist)
        for r in reads:
            self.readers.setdefault(r, []).append(o)
        for w in writes:
            self.last_w[w] = o
            self.readers[w] = []
        self.ops[eng].append(o)
        return o

    def barrier(self):
        lasts = []
        for e in ENGS:
            for o in reversed(self.ops[e]):
                if o.fn is not None:
                    lasts.append(o)
                    break
        for hist in self.dma_hist:
            if hist:
                lasts.append(hist[-1])
        for e in ENGS:
            o = self.op(e, None)
            for d in lasts:
                if d.eng == e and not d.dma and e == 'pe':
                    continue
                if d not in o.deps:
                    o.deps.append(d)
                if not d.dma:
                    d.sig = True
        self.last_w = {}
        self.readers = {}

    def emit(self):
        nc = self.nc
        cnt = {e: 0 for e in ENGS}
        for e in ENGS:
            for o in self.ops[e]:
                if o.sig and not o.dma:
                    assert o.fn is not None
                    cnt[e] += 1
                    o.cnt = cnt[e]
        self.sig_counts = cnt
        with ExitStack() as st:
            sems = {e: st.enter_context(nc.semaphore("s_" + e)) for e in ENGS}
            dsems = [st.enter_context(nc.semaphore("d_%d" % i)) for i in range(self.n_dma_sems)]
            block = st.enter_context(nc.Block())

            def run(engname, engobj):
                seen = {}
                for o in self.ops[engname]:
                    waits = {}
                    for d in o.deps:
                        if d.dma:
                            key = ('d', d.dsem)
                            val = d.dval
                        else:
                            key = ('e', d.eng)
                            val = d.cnt
                        if seen.get(key, 0) >= val:
                            continue
                        if waits.get(key, 0) < val:
                            waits[key] = val
                    if o.dma and o.prev_dma is not None:
                        key = ('d', o.dsem)
                        val = o.prev_dma.dval
                        if seen.get(key, 0) < val and waits.get(key, 0) < val:
                            waits[key] = val
                    for key, val in waits.items():
                        sem = dsems[key[1]] if key[0] == 'd' else sems[key[1]]
                        engobj.wait_ge(sem, val)
                        seen[key] = val
                    if o.fn is None:
                        continue
                    ins = o.fn(engobj)
                    if o.dma:
                        ins.then_inc(dsems[o.dsem], 16)
                    elif o.sig:
                        ins.then_inc(sems[o.eng], 1)

            @block.tensor
            def _(e):
                run('pe', e)

            @block.scalar
            def _(e):
                run('act', e)

            @block.vector
            def _(e):
                run('dve', e)

            @block.gpsimd
            def _(e):
                run('pool', e)

            @block.sync
            def _(e):
                run('sp', e)


class Arena:
    def __init__(self, ap, n):
        self.ap = ap
        self.n = n
        self.off = 0

    def take(self, n, parts=128):
        assert self.off + n <= self.n, ("arena overflow", self.off, n, self.n)
        a = self.ap[0:parts, self.off:self.off + n]
        self.off += n
        return a

    def reset(self, off=0):
        self.off = off


class K:
    def __init__(self, nc, P):
        self.nc = nc
        self.P = P

    def mm(self, out, pairs, reads, writes, first=True, last=True):
        pairs = list(pairs)

        def fn(e):
            n = len(pairs)
            ins = None
            for i, (l, r) in enumerate(pairs):
                ins = e.matmul(out, l, r, start=(first and i == 0), stop=(last and i == n - 1))
            return ins
        self.P.op('pe', fn, reads, writes)

    def tr(self, out, in_, reads, writes):
        ident = self.identb
        p = in_.shape[0]
        self.P.op('pe', lambda e: e.transpose(out, in_, ident[0:p, 0:p]), list(reads) + ['const'], writes)

    def act(self, out, in_, func, reads, writes, bias=None, scale=None, accum_out=None):
        kw = {}
        if bias is not None:
            kw['bias'] = bias
        if scale is not None:
            kw['scale'] = scale
        if accum_out is not None:
            kw['accum_out'] = accum_out
        self.P.op('act', lambda e: e.activation(out, in_, func, **kw), reads, writes)

    def tt(self, out, in0, in1, op, reads, writes, eng='dve'):
        self.P.op(eng, lambda e: e.tensor_tensor(out, in0, in1, op=op), reads, writes)

    def ts(self, out, in0, s1, s2, op0, op1, reads, writes, eng='dve'):
        if s2 is None:
            self.P.op(eng, lambda e: e.tensor_scalar(out, in0, s1, None, op0=op0), reads, writes)
        else:
            self.P.op(eng, lambda e: e.tensor_scalar(out, in0, s1, s2, op0=op0, op1=op1), reads, writes)

    def stt(self, out, in0, sc, in1, op0, op1, reads, writes, eng='dve'):
        self.P.op(eng, lambda e: e.scalar_tensor_tensor(out, in0, sc, in1, op0=op0, op1=op1), reads, writes)

    def cp(self, out, in_, reads, writes, eng='dve'):
        if eng == 'act':
            self.P.op('act', lambda e: e.copy(out, in_), reads, writes)
        else:
            self.P.op(eng, lambda e: e.tensor_copy(out, in_), reads, writes)

    def dma(self, out, in_, reads, writes, eng='sp'):
        self.P.op(eng, lambda e: e.dma_start(out=out, in_=in_), reads, writes, dma=True)


def build_program(nseq, layers, final_ln_only=False):
    nc = bass.Bass("TRN2", target_bir_lowering=False)
    P = Prog(nc)
    k = K(nc, P)
    dt_in = {}

    def din(name, shape, dtype=F32):
        t = nc.dram_tensor(name, list(shape), dtype, kind="ExternalInput").ap()
        dt_in[name] = t
        return t

    x_h = din("x", [nseq, S, D])
    pos_h = din("pos", [nseq, 1, S], I32)
    lng_h = din("ln_g", [DEPTH, 1, D])
    lnb_h = din("ln_b", [DEPTH, 1, D])
    cst_h = din("consts", [128, 7 * 128])
    gWA_h = din("gla_WA", [2, 4, 128, 8 * 768])
    gWZ_h = din("gla_WZ", [2, 4, 128, 8 * 512])
    gWO_h = din("gla_WO", [2, 4, 128, 4 * 1024])
    gWGL_h = din("gla_WGL", [2, 128, 8 * 32])
    gwg_h = din("gla_wg", [2, 4, 33, 256])
    ggn_h = din("gla_gn", [2, 128, 16])
    mWI_h = din("mla_WI", [2, 128, 8 * 896])
    mWH_h = din("mla_WH", [2, 16, 128, 3328])
    mqg_h = din("mla_qg", [2, 128, 3])
    mkg_h = din("mla_kg", [2, 128, 2])
    out_h = nc.dram_tensor("out", [nseq, S, D], F32, kind="ExternalOutput").ap()

    N16 = 47872
    N32 = 3712
    with ExitStack() as st:
        sb = lambda n, s, d: st.enter_context(nc.sbuf_tensor(n, s, d))
        x32 = sb("x32", [128, NT, D], F32)
        xT = sb("xT", [128, KC, S], BF16)
        cstb = sb("cstb", [128, 6 * 128], BF16)
        cstf = sb("cstf", [128, 3 * 128], F32)
        a16t = sb("a16", [128, N16], BF16)
        a32t = sb("a32", [128, N32], F32)
        ps = [st.enter_context(nc.psum_tensor("ps%d" % i, [128, 512], F32)) for i in range(7)]
        psT = st.enter_context(nc.psum_tensor("psT", [128, 1024], BF16))
        A16 = Arena(a16t, N16)
        A32 = Arena(a32t, N32)

        k.identb = cstb[:, 0:128]
        trif = cstb[:, 128:256]
        trib = cstb[:, 256:384]
        onesb = cstb[:, 640:768]
        maskfb = cstf[:, 0:256]
        invf = cstf[:, 256:257]
        shiftS = cstf[:, 257:258]

        k.dma(cstb[:, :], cst_h[:, 0:768], [], ['const'], eng='pool')
        k.dma(cstf[:, 0:256], cst_h[:, 384:640], [], ['const'])
        k.dma(cstf[:, 256:384], cst_h[:, 768:896], [], ['const'])

        psT_alt = [psT[:, :], ps[6][:, :].bitcast(BF16)]
        psT_names = ['psT', 'ps6']

        def make_xT_a(t):
            p = t % 2
            xb = k.xbs[p]
            pT = psT_alt[p]
            k.cp(xb, x32[:, t, :], ['x%d' % t], ['xb%d' % p], eng='act')
            for kc in range(KC):
                k.tr(pT[:, kc * 128:(kc + 1) * 128], xb[:, kc * 128:(kc + 1) * 128], ['xb%d' % p], [psT_names[p]])

        def make_xT_b(t):
            p = t % 2
            tokl = slice(t * 128, (t + 1) * 128)
            k.cp(xT[:, :, tokl], psT_alt[p].rearrange("p (k t) -> p k t", k=KC), [psT_names[p]], ['xT%d' % t])

        def make_xT(t):
            make_xT_a(t)
            make_xT_b(t)

        def layer_norm_phase(i, seq, last, nxt_layer=None):
            A32.reset()
            lng = A32.take(D)
            lnb = A32.take(D)
            sts = [A32.take(24) for _ in range(4)]
            k.xbs = [a16t[:, N16 - 2 * D:N16 - D], a16t[:, N16 - D:N16]]
            k.dma(lng, lng_h[i].partition_broadcast(128), [], ['lng'])
            k.dma(lnb, lnb_h[i].partition_broadcast(128), [], ['lnb'])
            if nxt_layer is not None:
                sv = A32.off
                prefetch(*nxt_layer)
                A32.reset(sv)

            def s1(t):
                st6 = sts[t % 4]
                sn = 'st%d' % (t % 4)
                xr = 'x%d' % t
                xt_ = x32[:, t, :]
                P.op('dve', lambda e: e.bn_stats(st6[:, 0:6], xt_[:, 0:512]), [xr], [sn + 'a'])
                P.op('dve', lambda e: e.bn_stats(st6[:, 6:12], xt_[:, 512:1024]), [xr], [sn + 'b'])
                P.op('dve', lambda e: e.bn_aggr(st6[:, 12:14], st6[:, 0:12]), [sn + 'a', sn + 'b'], [sn + 'mv'])
                k.act(st6[:, 14:15], st6[:, 13:14], AF.Ln, [sn + 'mv'], [sn + 'ln'], bias=EPS)
                k.act(st6[:, 15:16], st6[:, 14:15], AF.Exp, [sn + 'ln'], [sn + 'rs'], scale=-0.5)

            def s2(t):
                st6 = sts[t % 4]
                sn = 'st%d' % (t % 4)
                xr = 'x%d' % t
                xt_ = x32[:, t, :]
                k.stt(st6[:, 16:17], st6[:, 12:13], -1.0, st6[:, 15:16], ALU.mult, ALU.mult, [sn + 'mv', sn + 'rs'], [sn + 'nb'])
                k.act(xt_, xt_, AF.Identity, [xr, sn + 'rs', sn + 'nb'], [xr], bias=st6[:, 16:17], scale=st6[:, 15:16])

            def s3(t):
                xr = 'x%d' % t
                xt_ = x32[:, t, :]
                k.tt(xt_, xt_, lng, ALU.mult, [xr, 'lng'], [xr])
                k.tt(xt_, xt_, lnb, ALU.add, [xr, 'lnb'], [xr])

            def s4(t):
                xr = 'x%d' % t
                if last:
                    k.dma(out_h[seq, t * 128:(t + 1) * 128, :], x32[:, t, :], [xr], ['out%d' % t])
                else:
                    make_xT_a(t)

            nxt_seq = (seq + 1) if (last and seq + 1 < nseq) else None
            for n in range(NT + 9):
                if n < NT:
                    s1(n)
                if 0 <= n - 1 < NT:
                    s2(n - 1)
                if 0 <= n - 2 < NT:
                    s3(n - 2)
                if 0 <= n - 3 < NT:
                    s4(n - 3)
                if 0 <= n - 4 < NT and not last:
                    make_xT_b(n - 4)
                if nxt_seq is not None:
                    if 0 <= n - 4 < NT:
                        t = n - 4
                        k.dma(x32[:, t, :], x_h[nxt_seq, t * 128:(t + 1) * 128, :], [], ['x%d' % t], eng='pool')
                    if 0 <= n - 7 < NT:
                        make_xT_a(n - 7)
                    if 0 <= n - 8 < NT:
                        make_xT_b(n - 8)

        def gla_layer(j, i, seq):
            A16.reset()
            A32.reset()
            WA = A16.take(8 * 768).rearrange("p (k f) -> p k f", k=8)
            WZ = A16.take(8 * 512).rearrange("p (k f) -> p k f", k=8)
            WO = A16.take(4 * 1024).rearrange("p (k f) -> p k f", k=4)
            qtf = A16.take(S)
            qtb = A16.take(S)
            ktf = A16.take(S)
            ktb = A16.take(S)
            ktok = A16.take(NT * 128).rearrange("p (t f) -> p t f", t=NT)
            v = A16.take(NT * 512).rearrange("p (t f) -> p t f", t=NT)
            snap = A16.take(NT * 512).rearrange("p (t f) -> p t f", t=NT)
            glrT = A16.take(S)
            wg = A16.take(256)
            WGL = A16.take(8 * 32).rearrange("p (k f) -> p k f", k=8)
            sp2 = A16.take(512)
            sT = [A16.take(256), A16.take(256)]
            sbf = [A16.take(512), A16.take(512)]
            ktmp = [A16.take(128), A16.take(128)]
            og = [A16.take(512), A16.take(512)]
            ogT = [A16.take(512).rearrange("p (c t) -> p c t", c=4), A16.take(512).rearrange("p (c t) -> p c t", c=4)]
            etmp = A32.take(512)
            eq = [A32.take(256), A32.take(256)]
            ek = [A32.take(256), A32.take(256)]
            Rf = A32.take(512)
            Rb = A32.take(512)
            ebf = A32.take(16)
            ebb = A32.take(16)
            sg = A32.take(512)
            zs = A32.take(512)
            sm = A32.take(8)
            gn = A32.take(16)
            lnscale = math.log(128 ** -0.5)
            vbank = [2, 6]

            if k.prefetch_only:
                k.dma(WGL, gWGL_h[j].rearrange("p (k f) -> p k f", k=8), [], ['WGL'], eng='pool')
                k.dma(gn, ggn_h[j], [], ['gn'])
                k.dma(WA, gWA_h[j, 0].rearrange("p (k f) -> p k f", k=8), [], ['WA'], eng='pool')
                k.dma(wg[0:33, :], gwg_h[j, 0], [], ['wg'], eng='pool')
                k.dma(WZ, gWZ_h[j, 0].rearrange("p (k f) -> p k f", k=8), [], ['WZ'], eng='pool')
                k.dma(WO, gWO_h[j, 0].rearrange("p (k f) -> p k f", k=4), [], ['WO'], eng='pool')
                return
            P.op('dve', lambda e: e.memset(glrT[32:33, :], 1.0), [], ['glrT'])
            for g in range(4):
                tg = slice(g * 512, (g + 1) * 512)
                k.mm(ps[0][0:32, :], [(WGL[:, kc, :], xT[:, kc, tg]) for kc in range(KC)],
                     ['WGL', 'xT'], ['ps0'])
                k.cp(glrT[0:32, tg], ps[0][0:32, :], ['ps0'], ['glrT'])

            for h in range(4):
                if h > 0:
                    k.dma(wg[0:33, :], gwg_h[j, h], [], ['wg'], eng='pool')
                    k.dma(WZ, gWZ_h[j, h].rearrange("p (k f) -> p k f", k=8), [], ['WZ'], eng='pool')
                    k.dma(WO, gWO_h[j, h].rearrange("p (k f) -> p k f", k=4), [], ['WO'], eng='pool')

                for c in range(4):
                    k.ts(WO[:, c, :], WO[:, c, :], gn[:, 4 * h + c:4 * h + c + 1], None, ALU.mult, None, ['WO', 'gn'], ['WO'])

                def gate(pr):
                    for u in (0, 1):
                        tl = slice((2 * pr + u) * 128, (2 * pr + u + 1) * 128)
                        k.mm(ps[3][:, u * 256:(u + 1) * 256], [(glrT[0:33, tl], wg[0:33, :])], ['glrT', 'wg'], ['ps3'])
                    k.act(etmp, ps[3][:, :], AF.Exp, ['ps3'], ['etmp'], scale=-1.0)
                    k.act(sp2, etmp, AF.Ln, ['etmp'], ['sp2'], bias=1.0)

                def bw_state(t):
                    k.mm(ps[5][:, :], [(ktmp[t % 2], v[:, t, :])], ['ktmp%d' % (t % 2), 'v%d' % t], ['ps5'])
                    if t == NT - 1:
                        k.cp(Rb, ps[5][:, :], ['ps5'], ['Rb'])
                    else:
                        k.stt(Rb, Rb, ebb[:, t + 1:t + 2], ps[5][:, :], ALU.mult, ALU.add,
                              ['Rb', 'ebb%d' % (t + 1), 'ps5'], ['Rb'])
                    k.act(snap[:, t - 1, :], Rb, AF.Identity, ['Rb', 'ebb%d' % t], ['snap%d' % (t - 1)],
                          scale=ebb[:, t:t + 1])

                gate(7)
                for pr in reversed(range(8)):
                    ta, tb = 2 * pr + 1, 2 * pr
                    pq = ps[pr % 2]
                    pqn = 'ps%d' % (pr % 2)
                    tp2 = slice(pr * 256, (pr + 1) * 256)
                    k.mm(pq[:, 0:256], [(WA[:, kc, 0:128], xT[:, kc, tp2]) for kc in range(KC)], ['WA', 'xT'], [pqn])
                    k.mm(pq[:, 256:512], [(WA[:, kc, 128:256], xT[:, kc, tp2]) for kc in range(KC)], ['WA', 'xT'], [pqn])
                    if ta + 2 <= NT - 1:
                        bw_state(ta + 2)
                    for t in (ta, tb):
                        u = t - tb
                        par = t % 2
                        tl = slice(t * 128, (t + 1) * 128)
                        pb = ps[4][:, par * 256:(par + 1) * 256]
                        pbn = 'ps4_%d' % par
                        k.mm(pb[:, 0:128], [(sp2[:, u * 256:u * 256 + 128], trif)], ['sp2', 'const'], [pbn])
                        k.mm(pb[:, 128:256], [(sp2[:, u * 256 + 128:u * 256 + 256], trib)], ['sp2', 'const'], [pbn])
                        k.act(eq[par], pb, AF.Exp, [pbn], ['eq%d' % par], bias=lnscale)
                        k.act(ek[par], pb, AF.Exp, [pbn], ['ek%d' % par], scale=-1.0)
                        k.act(ebf[:, t:t + 1], pb[:, 127:128], AF.Exp, [pbn], ['ebf%d' % t])
                        k.act(ebb[:, t:t + 1], pb[:, 128:129], AF.Exp, [pbn], ['ebb%d' % t])
                        c0 = u * 128
                        k.tt(qtf[:, tl], pq[:, c0:c0 + 128], eq[par][:, 0:128], ALU.mult, [pqn, 'eq%d' % par], ['q%d' % t])
                        k.tt(qtb[:, tl], pq[:, c0:c0 + 128], eq[par][:, 128:256], ALU.mult, [pqn, 'eq%d' % par], ['q%d' % t])
                        k.tt(ktf[:, tl], pq[:, 256 + c0:256 + c0 + 128], ek[par][:, 0:128], ALU.mult, [pqn, 'ek%d' % par], ['k%d' % t])
                        k.tt(ktb[:, tl], pq[:, 256 + c0:256 + c0 + 128], ek[par][:, 128:256], ALU.mult, [pqn, 'ek%d' % par], ['k%d' % t])
                    if pr > 0:
                        gate(pr - 1)
                    for t in (ta, tb):
                        tl = slice(t * 128, (t + 1) * 128)
                        vb = vbank[t % 2]
                        k.mm(ps[vb][:, :], [(xT[:, kc, tl], WA[:, kc, 256:768]) for kc in range(KC)], ['WA', 'xT'], ['ps%d' % vb])
                        k.cp(v[:, t, :], ps[vb][:, :], ['ps%d' % vb], ['v%d' % t], eng='act')
                        if t == ta and tb + 2 <= NT - 1:
                            bw_state(tb + 2)
                    for t in (ta, tb):
                        tl = slice(t * 128, (t + 1) * 128)
                        par = t % 2
                        pc = psT[:, par * 256:(par + 1) * 256]
                        pcn = 'psT_%d' % par
                        k.tr(pc[:, 0:128], ktf[:, tl], ['k%d' % t], [pcn])
                        k.tr(pc[:, 128:256], ktb[:, tl], ['k%d' % t], [pcn])
                        k.cp(ktok[:, t, :], pc[:, 0:128], [pcn], ['ktok%d' % t])
                        k.cp(ktmp[par], pc[:, 128:256], [pcn], ['ktmp%d' % par])
                bw_state(1)
                if h < 3:
                    k.dma(WA, gWA_h[j, h + 1].rearrange("p (k f) -> p k f", k=8), [], ['WA'], eng='pool')

                def stage_a_pe(t):
                    tl = slice(t * 128, (t + 1) * 128)
                    par = t % 2
                    po = ps[3] if par == 0 else ps[6]
                    pon = 'ps3' if par == 0 else 'ps6'
                    k.mm(ps[0][:, 0:128], [(ktf[:, tl], qtf[:, tl])], ['k%d' % t, 'q%d' % t], ['ps0'])
                    k.mm(ps[0][:, 128:256], [(ktb[:, tl], qtb[:, tl])], ['k%d' % t, 'q%d' % t], ['ps0'])
                    k.tt(sT[par], ps[0][:, 0:256], maskfb, ALU.mult, ['ps0', 'const'], ['sT%d' % par])
                    if t < NT - 1:
                        k.mm(ps[1][:, :], [(ktok[:, t, :], v[:, t, :])], ['ktok%d' % t, 'v%d' % t], ['ps1'])
                        if t == 0:
                            k.cp(Rf, ps[1][:, :], ['ps1'], ['Rf'])
                        else:
                            k.stt(Rf, Rf, ebf[:, t - 1:t], ps[1][:, :], ALU.mult, ALU.add,
                                  ['Rf', 'ebf%d' % (t - 1), 'ps1'], ['Rf'])
                        k.act(sbf[par], Rf, AF.Identity, ['Rf', 'ebf%d' % t], ['sbf%d' % par], scale=ebf[:, t:t + 1])
                    k.mm(ps[2][:, :], [(xT[:, kc, tl], WZ[:, kc, :]) for kc in range(KC)], ['WZ', 'xT'], ['ps2'])
                    k.cp(etmp, ps[2][:, :], ['ps2'], ['etmp'])
                    k.act(sg, etmp, AF.Exp, ['etmp'], ['sg'], scale=-1.0)
                    k.act(sg, sg, AF.Ln, ['sg'], ['sg'], bias=1.0)
                    k.act(sg, sg, AF.Exp, ['sg'], ['sg'], scale=-1.0)
                    pairs = []
                    rd = ['sT%d' % par, 'v%d' % t, 'q%d' % t]
                    if t > 0:
                        pairs.append((qtf[:, tl], sbf[(t - 1) % 2]))
                        rd.append('sbf%d' % ((t - 1) % 2))
                    pairs.append((sT[par][:, 0:128], v[:, t, :]))
                    if t < NT - 1:
                        pairs.append((qtb[:, tl], snap[:, t, :]))
                        rd.append('snap%d' % t)
                    pairs.append((sT[par][:, 128:256], v[:, t, :]))
                    k.mm(po[:, :], pairs, rd, [pon])
                    P.op('act', lambda e: e.memzero(sm[:, 0:1]), [], ['ss'])
                    k.act(og[par], po[:, :], AF.Square, [pon], ['og%d' % par, 'ss'], accum_out=sm[:, 0:1])
                    k.act(sm[:, 1:2], sm[:, 0:1], AF.Ln, ['ss'], ['lnv'], bias=EPS, scale=1.0 / 512)
                    k.act(sm[:, 2 + par:3 + par], sm[:, 1:2], AF.Exp, ['lnv'], ['rstd%d' % par], scale=-0.5)

                def stage_a_tail(t):
                    par = t % 2
                    po = ps[3] if par == 0 else ps[6]
                    pon = 'ps3' if par == 0 else 'ps6'
                    k.tt(zs, etmp, sg, ALU.mult, ['etmp', 'sg'], ['zs'])
                    k.stt(og[par], po[:, :], sm[:, 2 + par:3 + par], zs, ALU.mult, ALU.mult, [pon, 'rstd%d' % par, 'zs'], ['og%d' % par])

                def stage_b1(t):
                    par = t % 2
                    for c in range(4):
                        k.tr(psT[:, 512 + c * 128:512 + (c + 1) * 128], og[par][:, c * 128:(c + 1) * 128], ['og%d' % par], ['psTb'])
                    k.cp(ogT[par], psT[:, 512:1024].rearrange("p (c t) -> p c t", c=4), ['psTb'], ['ogT%d' % par])

                def stage_b2(t):
                    par = t % 2
                    xr = 'x%d' % t
                    k.mm(ps[4][:, :], [(ogT[par][:, c, :], WO[:, c, 0:512]) for c in range(4)], ['ogT%d' % par, 'WO'], ['ps4_0', 'ps4_1'])
                    k.mm(ps[5][:, :], [(ogT[par][:, c, :], WO[:, c, 512:1024]) for c in range(4)], ['ogT%d' % par, 'WO'], ['ps5'])
                    for half, pb in ((0, 4), (1, 5)):
                        xs = x32[:, t, half * 512:(half + 1) * 512]
                        pbn = ['ps4_0', 'ps4_1'] if pb == 4 else ['ps5']
                        if h == 0:
                            k.stt(xs, xs, ALPHA, ps[pb][:, :], ALU.mult, ALU.add, [xr] + pbn, [xr])
                        else:
                            k.tt(xs, xs, ps[pb][:, :], ALU.add, [xr] + pbn, [xr])

                stage_a_pe(0)
                stage_a_tail(0)
                for t in range(NT):
                    if t + 1 < NT:
                        stage_a_pe(t + 1)
                    stage_b1(t)
                    if t >= 1:
                        stage_b2(t - 1)
                    if t + 1 < NT:
                        stage_a_tail(t + 1)
                stage_b2(NT - 1)

        def mla_layer(j, i, seq):
            A16.reset()
            A32.reset()
            WI = A16.take(8 * 896).rearrange("p (k f) -> p k f", k=8)
            cqn = A16.take(3 * S).rearrange("p (k t) -> p k t", k=3)
            ckvn = A16.take(2 * S).rearrange("p (k t) -> p k t", k=2)
            Kpp = A16.take(S)
            CC = A16.take(S)
            SS = A16.take(S)
            qn = A16.take(S)
            qr = A16.take(S)
            kn = A16.take(S)
            vv = A16.take(NT * 128)
            silu = A16.take(S)
            ogT = A16.take(S)
            WHb = [A16.take(3328), A16.take(3328)]
            PT = [A16.take(512), A16.take(512), A16.take(512)]
            sq = [A16.take(512), A16.take(512), A16.take(512)]
            sq3 = A16.take(512)
            f0 = A32.take(512)
            f1 = A32.take(512)
            f2 = A32.take(512)
            f3 = A32.take(512)
            rstd = A32.take(512)
            rec = A32.take(512)
            qg = A32.take(3)
            kg = A32.take(2)
            sc = 192 ** -0.5
            TWO_PI = 2.0 * math.pi

            if k.prefetch_only:
                k.dma(WI, mWI_h[j].rearrange("p (k f) -> p k f", k=8), [], ['WI'], eng='pool')
                k.dma(qg, mqg_h[j], [], ['qg'])
                k.dma(kg, mkg_h[j], [], ['kg'])
                k.dma(WHb[0], mWH_h[j, 0], [], ['WH0'], eng='pool')
                return
            f0i = f0.bitcast(I32)
            f3i = f3.bitcast(I32)
            for g in range(4):
                tg = slice(g * 512, (g + 1) * 512)
                for (c0, nch, dst, gcol, gname, width) in ((0, 3, cqn, qg, 'qg', 384.0), (384, 2, ckvn, kg, 'kg', 256.0)):
                    for c in range(nch):
                        k.mm(ps[c][:, :], [(WI[:, kc, c0 + c * 128:c0 + (c + 1) * 128], xT[:, kc, tg]) for kc in range(KC)],
                             ['WI', 'xT'], ['ps%d' % c])
                    for c in range(nch):
                        k.act(sq[c], ps[c][:, :], AF.Square, ['ps%d' % c], ['sq%d' % c])
                        k.mm(ps[3][:, :], [(onesb, sq[c])], ['sq%d' % c, 'const'], ['ps3'], first=(c == 0), last=(c == nch - 1))
                    k.act(rstd, ps[3][:, :], AF.Ln, ['ps3'], ['rstd'], bias=EPS, scale=1.0 / width)
                    k.act(rstd, rstd, AF.Exp, ['rstd'], ['rstd'], scale=-0.5)
                    for c in range(nch):
                        k.stt(dst[:, c, tg], ps[c][:, :], gcol[:, c:c + 1], rstd, ALU.mult, ALU.mult,
                              ['ps%d' % c, gname, 'rstd'], ['lowrank'])
                k.mm(ps[4][:, :], [(WI[:, kc, 640:768], xT[:, kc, tg]) for kc in range(KC)], ['WI', 'xT'], ['ps4'])
                k.mm(ps[5][:, :], [(WI[:, kc, 768:896], xT[:, kc, tg]) for kc in range(KC)], ['WI', 'xT'], ['ps5'])
                k.dma(f0i, pos_h[seq, 0:1, tg].partition_broadcast(128), [], ['f0'])
                k.cp(f1, f0i, ['f0'], ['f1'])
                k.ts(f1, f1, invf, 1.0 / TWO_PI, ALU.mult, ALU.mult, ['f1', 'const'], ['f1'])
                for tab, shift in ((CC, 0.25), (SS, shiftS)):
                    k.ts(f2, f1, shift, None, ALU.add, None, ['f1', 'const'], ['f2'])
                    k.cp(f3i, f2, ['f2'], ['f3'])
                    k.cp(f0, f3i, ['f3'], ['f0'])
                    k.tt(f2, f2, f0, ALU.subtract, ['f2', 'f0'], ['f2'])
                    k.ts(f0, f2, 0.5, None, ALU.is_gt, None, ['f2'], ['f0'])
                    k.tt(f2, f2, f0, ALU.subtract, ['f2', 'f0'], ['f2'])
                    k.ts(f0, f2, -0.5, None, ALU.is_lt, None, ['f2'], ['f0'])
                    k.tt(f2, f2, f0, ALU.add, ['f2', 'f0'], ['f2'])
                    k.act(tab[:, tg], f2, AF.Sin, ['f2'], ['tab'], scale=TWO_PI)
                k.tt(f0, ps[4][:, :], CC[:, tg], ALU.mult, ['ps4', 'tab'], ['f0'])
                k.tt(f1, ps[5][:, :], SS[:, tg], ALU.mult, ['ps5', 'tab'], ['f1'])
                k.tt(Kpp[:, tg], f0, f1, ALU.add, ['f0', 'f1'], ['Kpp'])

            def proj_chunks(h):
                WH = WHb[h % 2]
                whn = 'WH%d' % (h % 2)
                wuq = WH[:, 0:768].rearrange("p (k f) -> p k f", k=3)
                wukv = WH[:, 768:1280].rearrange("p (k f) -> p k f", k=2)
                wz = WH[:, 1280:2304].rearrange("p (k f) -> p k f", k=8)
                out = []
                for g in range(4):
                    tg = slice(g * 512, (g + 1) * 512)

                    def c_qn(tg=tg):
                        k.mm(ps[0][:, :], [(wuq[:, kc, 0:128], cqn[:, kc, tg]) for kc in range(3)], [whn, 'lowrank'], ['ps0'])
                        k.cp(qn[:, tg], ps[0][:, :], ['ps0'], ['qn'], eng='act')

                    def c_qr(tg=tg):
                        k.mm(ps[1][:, :], [(wuq[:, kc, 128:256], cqn[:, kc, tg]) for kc in range(3)], [whn, 'lowrank'], ['ps1'])
                        k.tt(qr[0:64, tg], ps[1][0:64, :], CC[0:64, tg], ALU.mult, ['ps1', 'tab'], ['qr'])
                        k.tt(qr[64:128, tg], ps[1][64:128, :], SS[64:128, tg], ALU.mult, ['ps1', 'tab'], ['qr'])

                    def c_kn(tg=tg):
                        k.mm(ps[0][:, :], [(wukv[:, kc, 0:128], ckvn[:, kc, tg]) for kc in range(2)], [whn, 'lowrank'], ['ps0'])
                        k.cp(kn[:, tg], ps[0][:, :], ['ps0'], ['kn'], eng='act')

                    def c_z(tg=tg):
                        k.mm(ps[2][:, :], [(wz[:, kc, :], xT[:, kc, tg]) for kc in range(KC)], [whn, 'xT'], ['ps2'])
                        k.act(f0, ps[2][:, :], AF.Exp, ['ps2'], ['f0'], scale=-1.0)
                        k.act(f0, f0, AF.Ln, ['f0'], ['f0'], bias=1.0)
                        k.act(f0, f0, AF.Exp, ['f0'], ['f0'], scale=-1.0)
                        k.tt(silu[:, tg], ps[2][:, :], f0, ALU.mult, ['ps2', 'f0'], ['silu'])

                    def c_v(g=g):
                        for u in range(4):
                            tl = slice((4 * g + u) * 128, (4 * g + u + 1) * 128)
                            k.mm(ps[1][:, u * 128:(u + 1) * 128], [(ckvn[:, kc, tl], wukv[:, kc, 128:256]) for kc in range(2)],
                                 [whn, 'lowrank'], ['ps1'])
                        k.cp(vv[:, g * 512:(g + 1) * 512], ps[1][:, :], ['ps1'], ['vv'])

                    out += [c_qn, c_qr, c_z, c_kn, c_v]
                return out

            for c in proj_chunks(0):
                c()
            for h in range(16):
                WH = WHb[h % 2]
                whn = 'WH%d' % (h % 2)
                if h < 15:
                    k.dma(WHb[(h + 1) % 2], mWH_h[j, h + 1], [], ['WH%d' % ((h + 1) % 2)], eng='pool')
                wo = WH[:, 2304:3328]
                blocks = [(g, kt) for g in range(4) for kt in range(NT)]
                Sb = [5, 6, 0]
                accD = [rstd, f2]
                accDn = ['rstd', 'f2']

                def emit_S(bi):
                    g, kt = blocks[bi]
                    qs = slice(g * 512, (g + 1) * 512)
                    ktl = slice(kt * 128, (kt + 1) * 128)
                    b = bi % 3
                    pS = ps[Sb[b]]
                    k.mm(pS[:, :], [(kn[:, ktl], qn[:, qs]), (Kpp[:, ktl], qr[:, qs])], ['kn', 'qn', 'Kpp', 'qr'], ['ps%d' % Sb[b]])
                    k.act(PT[b], pS[:, :], AF.Exp, ['ps%d' % Sb[b]], ['PT%d' % b], scale=sc)

                def finish(g):
                    qs = slice(g * 512, (g + 1) * 512)
                    pO, pD = (3, 4) if g % 2 == 0 else (1, 2)
                    k.mm(ps[pD][:, :], [(onesb, sq[0]), (onesb, sq[1])], ['const', 'sq0', 'sq1'], ['ps%d' % pD], first=False, last=True)
                    k.act(rec, ps[pD][:, :], AF.Ln, ['ps%d' % pD], ['rec'])
                    k.act(rec, rec, AF.Exp, ['rec'], ['rec'], scale=-1.0)
                    k.tt(f1, ps[pO][:, :], rec, ALU.mult, ['ps%d' % pO, 'rec'], ['f1'])
                    k.tt(ogT[:, qs], f1, silu[:, qs], ALU.mult, ['f1', 'silu'], ['ogT%d' % g])

                emit_S(0)
                emit_S(1)
                pending = []
                for bi, (g, kt) in enumerate(blocks):
                    if bi + 2 < len(blocks):
                        emit_S(bi + 2)
                    ktl = slice(kt * 128, (kt + 1) * 128)
                    b = bi % 3
                    pO, pD = (3, 4) if g % 2 == 0 else (1, 2)
                    k.mm(ps[pO][:, :], [(vv[:, ktl], PT[b])], ['vv', 'PT%d' % b], ['ps%d' % pO], first=(kt == 0), last=(kt == NT - 1))
                    if kt % 2 == 0:
                        k.mm(ps[pD][:, :], [(onesb, PT[b])], ['const', 'PT%d' % b], ['ps%d' % pD], first=(kt == 0), last=False)
                    else:
                        acc, an = accD[g % 2], accDn[g % 2]
                        if kt == 1:
                            k.cp(acc, PT[b], ['PT%d' % b], [an])
                        else:
                            k.tt(acc, acc, PT[b], ALU.add, [an, 'PT%d' % b], [an])
                    if pending and pending[0][1] == bi:
                        finish(pending.pop(0)[0])
                    if kt == NT - 1:
                        k.cp(sq[0], accD[g % 2], [accDn[g % 2]], ['sq0'])
                        k.tt(sq[1], accD[g % 2], sq[0], ALU.subtract, [accDn[g % 2], 'sq0'], ['sq1'])
                        pending.append((g, bi + 4))
                while pending:
                    finish(pending.pop(0)[0])
                nxt = proj_chunks(h + 1) if h < 15 else []
                for t in range(NT):
                    tl = slice(t * 128, (t + 1) * 128)
                    xr = 'x%d' % t
                    for half in (0, 1):
                        pb = (5 + half) if t % 2 == 0 else (3 + half)
                        k.mm(ps[pb][:, :], [(ogT[:, tl], wo[:, half * 512:(half + 1) * 512])], ['ogT%d' % (t // 4), whn], ['ps%d' % pb])
                        xs = x32[:, t, half * 512:(half + 1) * 512]
                        xrh = xr + '_%d' % half
                        if h == 0:
                            k.stt(xs, xs, ALPHA, ps[pb][:, :], ALU.mult, ALU.add, [xrh, 'ps%d' % pb], [xrh])
                        else:
                            k.tt(xs, xs, ps[pb][:, :], ALU.add, [xrh, 'ps%d' % pb], [xrh])
                    if nxt:
                        nxt.pop(0)()
                    if nxt and t % 4 == 3:
                        nxt.pop(0)()
                while nxt:
                    nxt.pop(0)()

        def prefetch(kind, j):
            k.prefetch_only = True
            if kind == 'gla':
                gla_layer(j, None, None)
            else:
                mla_layer(j, None, None)
            k.prefetch_only = False

        for seq in range(nseq):
            P.barrier()
            if seq == 0:
                k.xbs = [a16t[:, N16 - 2 * D:N16 - D], a16t[:, N16 - D:N16]]
                prefetch(*layers[0][:2])
                for t in range(NT):
                    k.dma(x32[:, t, :], x_h[seq, t * 128:(t + 1) * 128, :], [], ['x%d' % t])
                    make_xT(t)
            for li, (kind, j, i) in enumerate(layers):
                P.barrier()
                k.prefetch_only = False
                if kind == 'gla':
                    gla_layer(j, i, seq)
                else:
                    mla_layer(j, i, seq)
                P.barrier()
                if li + 1 < len(layers):
                    nxt_layer = layers[li + 1][:2]
                elif seq + 1 < nseq:
                    nxt_layer = layers[0][:2]
                else:
                    nxt_layer = None
                layer_norm_phase(i, seq, last=(li == len(layers) - 1), nxt_layer=nxt_layer)
        P.barrier()
        P.emit()
    return nc


def host_consts():
    c = np.zeros((128, 7 * 128), np.float32)
    c[:, 0:128] = np.eye(128)
    jj = np.arange(128)[:, None]
    ii = np.arange(128)[None, :]
    c[:, 128:256] = np.where(jj <= ii, -1.0 / 16, 0.0)
    c[:, 256:384] = np.where(jj >= ii, -1.0 / 16, 0.0)
    c[:, 384:512] = (jj <= ii)
    c[:, 512:640] = (jj >= ii)
    c[:, 640:768] = 1.0
    inv_freq = (1.0 / (10000.0 ** (np.arange(0, 64, 2, dtype=np.float32) / 64))).astype(np.float32)
    c[:, 768] = inv_freq[np.arange(128) % 32]
    c[:, 769] = np.where((np.arange(128) // 32) % 2 == 0, 0.5, 0.0)
    return c


def host_layout(inp):
    f = np.float32
    d = {}
    d["ln_g"] = np.ascontiguousarray(inp["ln_g"], f).reshape(DEPTH, 1, D)
    d["ln_b"] = np.ascontiguousarray(inp["ln_b"], f).reshape(DEPTH, 1, D)
    d["consts"] = host_consts()
    w_in = np.asarray(inp["gla_w_in"], f)
    WA = np.zeros((2, 4, 128, 8, 768), f)
    WZ = np.zeros((2, 4, 128, 8, 512), f)
    for h in range(4):
        blk = np.concatenate([w_in[:, :, h * 128:(h + 1) * 128], w_in[:, :, 512 + h * 128:512 + (h + 1) * 128],
                              w_in[:, :, 1024 + h * 512:1024 + (h + 1) * 512]], axis=2)
        WA[:, h] = blk.reshape(2, 8, 128, 768).transpose(0, 2, 1, 3)
        zb = w_in[:, :, 3072 + h * 512:3072 + (h + 1) * 512]
        WZ[:, h] = zb.reshape(2, 8, 128, 512).transpose(0, 2, 1, 3)
    d["gla_WA"] = WA.reshape(2, 4, 128, 8 * 768)
    d["gla_WZ"] = WZ.reshape(2, 4, 128, 8 * 512)
    wo = np.asarray(inp["gla_w_out"], f)
    d["gla_WO"] = np.ascontiguousarray(wo.reshape(2, 4, 4, 128, 1024).transpose(0, 1, 3, 2, 4)).reshape(2, 4, 128, 4096)
    d["gla_WGL"] = np.ascontiguousarray(w_in[:, :, 5120:5152].reshape(2, 8, 128, 32).transpose(0, 2, 1, 3)).reshape(2, 128, 256)
    wgate = np.asarray(inp["gla_w_gate"], f)
    bgate = np.asarray(inp["gla_b_gate"], f)
    wg = np.zeros((2, 4, 33, 256), f)
    for h in range(4):
        wg[:, h, 0:16, 0:128] = wgate[:, 0, :, h * 128:(h + 1) * 128]
        wg[:, h, 16:32, 128:256] = wgate[:, 1, :, h * 128:(h + 1) * 128]
        wg[:, h, 32, 0:128] = bgate[:, 0, h * 128:(h + 1) * 128]
        wg[:, h, 32, 128:256] = bgate[:, 1, h * 128:(h + 1) * 128]
    d["gla_wg"] = wg
    d["gla_gn"] = np.ascontiguousarray(np.asarray(inp["gla_gn_g"], f).reshape(2, 16, 128).transpose(0, 2, 1))
    mw = np.asarray(inp["mla_w_in"], f)
    kr = mw[:, :, 640:704]
    krP = np.concatenate([kr[:, :, 32:64], kr[:, :, 0:32]], axis=2)
    wi = np.concatenate([mw[:, :, 0:640], kr, kr, krP, krP], axis=2)
    d["mla_WI"] = np.ascontiguousarray(wi.reshape(2, 8, 128, 896).transpose(0, 2, 1, 3)).reshape(2, 128, 8 * 896)
    uq = np.asarray(inp["mla_w_uq"], f).reshape(2, 3, 128, 16, 192)
    ukv = np.asarray(inp["mla_w_ukv"], f).reshape(2, 2, 128, 16, 256)
    mz = mw[:, :, 704:2752].reshape(2, 8, 128, 16, 128)
    mo = np.asarray(inp["mla_w_out"], f).reshape(2, 16, 128, 1024)
    WH = np.zeros((2, 16, 128, 3328), f)
    for h in range(16):
        qn = uq[:, :, :, h, 0:128]
        qr = uq[:, :, :, h, 128:192]
        qrP = np.concatenate([qr[..., 32:64], qr[..., 0:32]], axis=-1)
        blkq = np.concatenate([qn, qr, qrP], axis=-1)
        WH[:, h, :, 0:768] = blkq.transpose(0, 2, 1, 3).reshape(2, 128, 768)
        WH[:, h, :, 768:1280] = ukv[:, :, :, h, :].transpose(0, 2, 1, 3).reshape(2, 128, 512)
        WH[:, h, :, 1280:2304] = mz[:, :, :, h, :].transpose(0, 2, 1, 3).reshape(2, 128, 1024)
        WH[:, h, :, 2304:3328] = mo[:, h]
    d["mla_WH"] = WH
    d["mla_qg"] = np.ascontiguousarray(np.asarray(inp["mla_q_norm_g"], f).reshape(2, 3, 128).transpose(0, 2, 1))
    d["mla_kg"] = np.ascontiguousarray(np.asarray(inp["mla_kv_norm_g"], f).reshape(2, 2, 128).transpose(0, 2, 1))
    return d


ALL_LAYERS = [('gla', 0, 0), ('mla', 0, 1), ('gla', 1, 2), ('mla', 1, 3)]


FUSED = True


def _run(x, pos, shared, layers):
    nc = build_program(NSEQ, layers)
    in_maps = []
    for c in range(NCORES):
        m = dict(shared)
        m["x"] = np.ascontiguousarray(x[c * NSEQ:(c + 1) * NSEQ])
        m["pos"] = np.ascontiguousarray(pos[c * NSEQ:(c + 1) * NSEQ]).reshape(NSEQ, 1, S)
        in_maps.append(m)
    res = run_bass_kernel_spmd(nc, in_maps, core_ids=list(range(NCORES)))
    return np.concatenate([np.asarray(r["out"], np.float32) for r in res.results], axis=0)


def kernel(**inputs):
    x = np.asarray(inputs["x"], np.float32)
    pos = np.asarray(inputs["positions"], np.int32)
    shared = host_layout(inputs)
    if FUSED:
        return _run(x, pos, shared, ALL_LAYERS)
    for lay in ALL_LAYERS:
        x = _run(x, pos, shared, [lay])
    return x
```

```python
import math
from contextlib import ExitStack

import numpy as np
import concourse.bass as bass
import concourse.mybir as mybir
from concourse.bass_utils import run_bass_kernel_spmd

F32 = mybir.dt.float32
BF16 = mybir.dt.bfloat16
I32 = mybir.dt.int32
AF = mybir.ActivationFunctionType
ALU = mybir.AluOpType

ENGS = ('pe', 'act', 'dve', 'pool', 'sp')

S = 2048
D = 1024
NT = 16
KC = 8
DEPTH = 4
ALPHA = (2 * DEPTH) ** 0.25
EPS = 1e-5
NSEQ = 4
NCORES = 8


class Op:
    __slots__ = ('eng', 'fn', 'deps', 'sig', 'cnt', 'dma', 'dsem', 'dval', 'prev_dma')


class Prog:
    def __init__(self, nc, n_dma_sems=16, same_engine_sync=True):
        self.nc = nc
        self.ops = {e: [] for e in ENGS}
        self.last_w = {}
        self.readers = {}
        self.n_dma_sems = n_dma_sems
        self.dma_count = 0
        self.dma_hist = [[] for _ in range(n_dma_sems)]
        self.same_engine_sync = same_engine_sync

    def op(self, eng, fn, reads=(), writes=(), dma=False):
        o = Op()
        o.eng = eng
        o.fn = fn
        o.sig = False
        o.cnt = None
        o.dma = dma
        o.dsem = None
        o.dval = None
        o.prev_dma = None
        deps = []
        seen = set()

        def add(d, raw):
            if d is None or id(d) in seen:
                return
            if (not d.dma) and (not dma) and d.eng == eng:
                if eng == 'pe' or not self.same_engine_sync or not raw:
                    return
            seen.add(id(d))
            deps.append(d)

        for r in reads:
            add(self.last_w.get(r), True)
        for w in writes:
            add(self.last_w.get(w), False)
            for rd in self.readers.get(w, ()):
                add(rd, False)
        o.deps = deps
        for d in deps:
            if not d.dma:
                d.sig = True
        if dma:
            k = self.dma_count % self.n_dma_sems
            self.dma_count += 1
            hist = self.dma_hist[k]
            o.prev_dma = hist[-1] if hist else None
            hist.append(o)
            o.dsem = k
            o.dval = 16 * len(hist)
        for r in reads:
            self.readers.setdefault(r, []).append(o)
        for w in writes:
            self.last_w[w] = o
            self.readers[w] = []
        self.ops[eng].append(o)
        return o

    def barrier(self):
        lasts = []
        for e in ENGS:
            for o in reversed(self.ops[e]):
                if o.fn is not None:
                    lasts.append(o)
                    break
        for hist in self.dma_hist:
            if hist:
                lasts.append(hist[-1])
        for e in ENGS:
            o = self.op(e, None)
            for d in lasts:
                if d.eng == e and not d.dma and e == 'pe':
                    continue
                if d not in o.deps:
                    o.deps.append(d)
                if not d.dma:
                    d.sig = True
        self.last_w = {}
        self.readers = {}

    def emit(self):
        nc = self.nc
        cnt = {e: 0 for e in ENGS}
        for e in ENGS:
            for o in self.ops[e]:
                if o.sig and not o.dma:
                    assert o.fn is not None
                    cnt[e] += 1
                    o.cnt = cnt[e]
        self.sig_counts = cnt
        with ExitStack() as st:
            sems = {e: st.enter_context(nc.semaphore("s_" + e)) for e in ENGS}
            dsems = [st.enter_context(nc.semaphore("d_%d" % i)) for i in range(self.n_dma_sems)]
            block = st.enter_context(nc.Block())

            def run(engname, engobj):
                seen = {}
                for o in self.ops[engname]:
                    waits = {}
                    for d in o.deps:
                        if d.dma:
                            key = ('d', d.dsem)
                            val = d.dval
                        else:
                            key = ('e', d.eng)
                            val = d.cnt
                        if seen.get(key, 0) >= val:
                            continue
                        if waits.get(key, 0) < val:
                            waits[key] = val
                    if o.dma and o.prev_dma is not None:
                        key = ('d', o.dsem)
                        val = o.prev_dma.dval
                        if seen.get(key, 0) < val and waits.get(key, 0) < val:
                            waits[key] = val
                    for key, val in waits.items():
                        sem = dsems[key[1]] if key[0] == 'd' else sems[key[1]]
                        engobj.wait_ge(sem, val)
                        seen[key] = val
                    if o.fn is None:
                        continue
                    ins = o.fn(engobj)
                    if o.dma:
                        ins.then_inc(dsems[o.dsem], 16)
                    elif o.sig:
                        ins.then_inc(sems[o.eng], 1)

            @block.tensor
            def _(e):
                run('pe', e)

            @block.scalar
            def _(e):
                run('act', e)

            @block.vector
            def _(e):
                run('dve', e)

            @block.gpsimd
            def _(e):
                run('pool', e)

            @block.sync
            def _(e):
                run('sp', e)


class Arena:
    def __init__(self, ap, n):
        self.ap = ap
        self.n = n
        self.off = 0

    def take(self, n, parts=128):
        assert self.off + n <= self.n, ("arena overflow", self.off, n, self.n)
        a = self.ap[0:parts, self.off:self.off + n]
        self.off += n
        return a

    def reset(self, off=0):
        self.off = off


class K:
    def __init__(self, nc, P):
        self.nc = nc
        self.P = P

    def mm(self, out, pairs, reads, writes, first=True, last=True):
        pairs = list(pairs)

        def fn(e):
            n = len(pairs)
            ins = None
            for i, (l, r) in enumerate(pairs):
                ins = e.matmul(out, l, r, start=(first and i == 0), stop=(last and i == n - 1))
            return ins
        self.P.op('pe', fn, reads, writes)

    def tr(self, out, in_, reads, writes):
        ident = self.identb
        p = in_.shape[0]
        self.P.op('pe', lambda e: e.transpose(out, in_, ident[0:p, 0:p]), list(reads) + ['const'], writes)

    def act(self, out, in_, func, reads, writes, bias=None, scale=None, accum_out=None):
        kw = {}
        if bias is not None:
            kw['bias'] = bias
        if scale is not None:
            kw['scale'] = scale
        if accum_out is not None:
            kw['accum_out'] = accum_out
        self.P.op('act', lambda e: e.activation(out, in_, func, **kw), reads, writes)

    def tt(self, out, in0, in1, op, reads, writes, eng='dve'):
        self.P.op(eng, lambda e: e.tensor_tensor(out, in0, in1, op=op), reads, writes)

    def ts(self, out, in0, s1, s2, op0, op1, reads, writes, eng='dve'):
        if s2 is None:
            self.P.op(eng, lambda e: e.tensor_scalar(out, in0, s1, None, op0=op0), reads, writes)
        else:
            self.P.op(eng, lambda e: e.tensor_scalar(out, in0, s1, s2, op0=op0, op1=op1), reads, writes)

    def stt(self, out, in0, sc, in1, op0, op1, reads, writes, eng='dve'):
        self.P.op(eng, lambda e: e.scalar_tensor_tensor(out, in0, sc, in1, op0=op0, op1=op1), reads, writes)

    def cp(self, out, in_, reads, writes, eng='dve'):
        if eng == 'act':
            self.P.op('act', lambda e: e.copy(out, in_), reads, writes)
        else:
            self.P.op(eng, lambda e: e.tensor_copy(out, in_), reads, writes)

    def dma(self, out, in_, reads, writes, eng='sp'):
        self.P.op(eng, lambda e: e.dma_start(out=out, in_=in_), reads, writes, dma=True)


def build_program(nseq, layers, final_ln_only=False):
    nc = bass.Bass("TRN2", target_bir_lowering=False)
    P = Prog(nc)
    k = K(nc, P)
    dt_in = {}

    def din(name, shape, dtype=F32):
        t = nc.dram_tensor(name, list(shape), dtype, kind="ExternalInput").ap()
        dt_in[name] = t
        return t

    x_h = din("x", [nseq, S, D])
    pos_h = din("pos", [nseq, 1, S], I32)
    lng_h = din("ln_g", [DEPTH, 1, D])
    lnb_h = din("ln_b", [DEPTH, 1, D])
    cst_h = din("consts", [128, 7 * 128])
    gWA_h = din("gla_WA", [2, 4, 128, 8 * 768])
    gWZ_h = din("gla_WZ", [2, 4, 128, 8 * 512])
    gWO_h = din("gla_WO", [2, 4, 128, 4 * 1024])
    gWGL_h = din("gla_WGL", [2, 128, 8 * 32])
    gwg_h = din("gla_wg", [2, 4, 33, 256])
    ggn_h = din("gla_gn", [2, 128, 16])
    mWI_h = din("mla_WI", [2, 128, 8 * 896])
    mWH_h = din("mla_WH", [2, 16, 128, 3328])
    mqg_h = din("mla_qg", [2, 128, 3])
    mkg_h = din("mla_kg", [2, 128, 2])
    out_h = nc.dram_tensor("out", [nseq, S, D], F32, kind="ExternalOutput").ap()

    N16 = 47872
    N32 = 3712
    with ExitStack() as st:
        sb = lambda n, s, d: st.enter_context(nc.sbuf_tensor(n, s, d))
        x32 = sb("x32", [128, NT, D], F32)
        xT = sb("xT", [128, KC, S], BF16)
        cstb = sb("cstb", [128, 6 * 128], BF16)
        cstf = sb("cstf", [128, 3 * 128], F32)
        a16t = sb("a16", [128, N16], BF16)
        a32t = sb("a32", [128, N32], F32)
        ps = [st.enter_context(nc.psum_tensor("ps%d" % i, [128, 512], F32)) for i in range(7)]
        psT = st.enter_context(nc.psum_tensor("psT", [128, 1024], BF16))
        A16 = Arena(a16t, N16)
        A32 = Arena(a32t, N32)

        k.identb = cstb[:, 0:128]
        trif = cstb[:, 128:256]
        trib = cstb[:, 256:384]
        onesb = cstb[:, 640:768]
        maskfb = cstf[:, 0:256]
        invf = cstf[:, 256:257]
        shiftS = cstf[:, 257:258]

        k.dma(cstb[:, :], cst_h[:, 0:768], [], ['const'], eng='pool')
        k.dma(cstf[:, 0:256], cst_h[:, 384:640], [], ['const'])
        k.dma(cstf[:, 256:384], cst_h[:, 768:896], [], ['const'])

        psT_alt = [psT[:, :], ps[6][:, :].bitcast(BF16)]
        psT_names = ['psT', 'ps6']

        def make_xT_a(t):
            p = t % 2
            xb = k.xbs[p]
            pT = psT_alt[p]
            k.cp(xb, x32[:, t, :], ['x%d' % t], ['xb%d' % p], eng='act')
            for kc in range(KC):
                k.tr(pT[:, kc * 128:(kc + 1) * 128], xb[:, kc * 128:(kc + 1) * 128], ['xb%d' % p], [psT_names[p]])

        def make_xT_b(t):
            p = t % 2
            tokl = slice(t * 128, (t + 1) * 128)
            k.cp(xT[:, :, tokl], psT_alt[p].rearrange("p (k t) -> p k t", k=KC), [psT_names[p]], ['xT%d' % t])

        def make_xT(t):
            make_xT_a(t)
            make_xT_b(t)

        def layer_norm_phase(i, seq, last, nxt_layer=None):
            A32.reset()
            lng = A32.take(D)
            lnb = A32.take(D)
            sts = [A32.take(24) for _ in range(4)]
            k.xbs = [a16t[:, N16 - 2 * D:N16 - D], a16t[:, N16 - D:N16]]
            k.dma(lng, lng_h[i].partition_broadcast(128), [], ['lng'])
            k.dma(lnb, lnb_h[i].partition_broadcast(128), [], ['lnb'])
            if nxt_layer is not None:
                sv = A32.off
                prefetch(*nxt_layer)
                A32.reset(sv)

            def s1(t):
                st6 = sts[t % 4]
                sn = 'st%d' % (t % 4)
                xr = 'x%d' % t
                xt_ = x32[:, t, :]
                P.op('dve', lambda e: e.bn_stats(st6[:, 0:6], xt_[:, 0:512]), [xr], [sn + 'a'])
                P.op('dve', lambda e: e.bn_stats(st6[:, 6:12], xt_[:, 512:1024]), [xr], [sn + 'b'])
                P.op('dve', lambda e: e.bn_aggr(st6[:, 12:14], st6[:, 0:12]), [sn + 'a', sn + 'b'], [sn + 'mv'])
                k.act(st6[:, 14:15], st6[:, 13:14], AF.Ln, [sn + 'mv'], [sn + 'ln'], bias=EPS)
                k.act(st6[:, 15:16], st6[:, 14:15], AF.Exp, [sn + 'ln'], [sn + 'rs'], scale=-0.5)

            def s2(t):
                st6 = sts[t % 4]
                sn = 'st%d' % (t % 4)
                xr = 'x%d' % t
                xt_ = x32[:, t, :]
                k.stt(st6[:, 16:17], st6[:, 12:13], -1.0, st6[:, 15:16], ALU.mult, ALU.mult, [sn + 'mv', sn + 'rs'], [sn + 'nb'])
                k.act(xt_, xt_, AF.Identity, [xr, sn + 'rs', sn + 'nb'], [xr], bias=st6[:, 16:17], scale=st6[:, 15:16])

            def s3(t):
                xr = 'x%d' % t
                xt_ = x32[:, t, :]
                k.tt(xt_, xt_, lng, ALU.mult, [xr, 'lng'], [xr])
                k.tt(xt_, xt_, lnb, ALU.add, [xr, 'lnb'], [xr])

            def s4(t):
                xr = 'x%d' % t
                if last:
                    k.dma(out_h[seq, t * 128:(t + 1) * 128, :], x32[:, t, :], [xr], ['out%d' % t])
                else:
                    make_xT_a(t)

            nxt_seq = (seq + 1) if (last and seq + 1 < nseq) else None
            for n in range(NT + 9):
                if n < NT:
                    s1(n)
                if 0 <= n - 1 < NT:
                    s2(n - 1)
                if 0 <= n - 2 < NT:
                    s3(n - 2)
                if 0 <= n - 3 < NT:
                    s4(n - 3)
                if 0 <= n - 4 < NT and not last:
                    make_xT_b(n - 4)
                if nxt_seq is not None:
                    if 0 <= n - 4 < NT:
                        t = n - 4
                        k.dma(x32[:, t, :], x_h[nxt_seq, t * 128:(t + 1) * 128, :], [], ['x%d' % t], eng='pool')
                    if 0 <= n - 7 < NT:
                        make_xT_a(n - 7)
                    if 0 <= n - 8 < NT:
                        make_xT_b(n - 8)

        def gla_layer(j, i, seq):
            A16.reset()
            A32.reset()
            WA = A16.take(8 * 768).rearrange("p (k f) -> p k f", k=8)
            WZ = A16.take(8 * 512).rearrange("p (k f) -> p k f", k=8)
            WO = A16.take(4 * 1024).rearrange("p (k f) -> p k f", k=4)
            qtf = A16.take(S)
            qtb = A16.take(S)
            ktf = A16.take(S)
            ktb = A16.take(S)
            ktok = A16.take(NT * 128).rearrange("p (t f) -> p t f", t=NT)
            v = A16.take(NT * 512).rearrange("p (t f) -> p t f", t=NT)
            snap = A16.take(NT * 512).rearrange("p (t f) -> p t f", t=NT)
            glrT = A16.take(S)
            wg = A16.take(256)
            WGL = A16.take(8 * 32).rearrange("p (k f) -> p k f", k=8)
            sp2 = A16.take(512)
            sT = [A16.take(256), A16.take(256)]
            sbf = [A16.take(512), A16.take(512)]
            ktmp = [A16.take(128), A16.take(128)]
            og = [A16.take(512), A16.take(512)]
            ogT = [A16.take(512).rearrange("p (c t) -> p c t", c=4), A16.take(512).rearrange("p (c t) -> p c t", c=4)]
            etmp = A32.take(512)
            eq = [A32.take(256), A32.take(256)]
            ek = [A32.take(256), A32.take(256)]
            Rf = A32.take(512)
            Rb = A32.take(512)
            ebf = A32.take(16)
            ebb = A32.take(16)
            sg = A32.take(512)
            zs = A32.take(512)
            sm = A32.take(8)
            gn = A32.take(16)
            lnscale = math.log(128 ** -0.5)
            vbank = [2, 6]

            if k.prefetch_only:
                k.dma(WGL, gWGL_h[j].rearrange("p (k f) -> p k f", k=8), [], ['WGL'], eng='pool')
                k.dma(gn, ggn_h[j], [], ['gn'])
                k.dma(WA, gWA_h[j, 0].rearrange("p (k f) -> p k f", k=8), [], ['WA'], eng='pool')
                k.dma(wg[0:33, :], gwg_h[j, 0], [], ['wg'], eng='pool')
                k.dma(WZ, gWZ_h[j, 0].rearrange("p (k f) -> p k f", k=8), [], ['WZ'], eng='pool')
                k.dma(WO, gWO_h[j, 0].rearrange("p (k f) -> p k f", k=4), [], ['WO'], eng='pool')
                return
            P.op('dve', lambda e: e.memset(glrT[32:33, :], 1.0), [], ['glrT'])
            for g in range(4):
                tg = slice(g * 512, (g + 1) * 512)
                k.mm(ps[0][0:32, :], [(WGL[:, kc, :], xT[:, kc, tg]) for kc in range(KC)],
                     ['WGL', 'xT'], ['ps0'])
                k.cp(glrT[0:32, tg], ps[0][0:32, :], ['ps0'], ['glrT'])

            for h in range(4):
                if h > 0:
                    k.dma(wg[0:33, :], gwg_h[j, h], [], ['wg'], eng='pool')
                    k.dma(WZ, gWZ_h[j, h].rearrange("p (k f) -> p k f", k=8), [], ['WZ'], eng='pool')
                    k.dma(WO, gWO_h[j, h].rearrange("p (k f) -> p k f", k=4), [], ['WO'], eng='pool')

                for c in range(4):
                    k.ts(WO[:, c, :], WO[:, c, :], gn[:, 4 * h + c:4 * h + c + 1], None, ALU.mult, None, ['WO', 'gn'], ['WO'])

                def gate(pr):
                    for u in (0, 1):
                        tl = slice((2 * pr + u) * 128, (2 * pr + u + 1) * 128)
                        k.mm(ps[3][:, u * 256:(u + 1) * 256], [(glrT[0:33, tl], wg[0:33, :])], ['glrT', 'wg'], ['ps3'])
                    k.act(etmp, ps[3][:, :], AF.Exp, ['ps3'], ['etmp'], scale=-1.0)
                    k.act(sp2, etmp, AF.Ln, ['etmp'], ['sp2'], bias=1.0)

                def bw_state(t):
                    k.mm(ps[5][:, :], [(ktmp[t % 2], v[:, t, :])], ['ktmp%d' % (t % 2), 'v%d' % t], ['ps5'])
                    if t == NT - 1:
                        k.cp(Rb, ps[5][:, :], ['ps5'], ['Rb'])
                    else:
                        k.stt(Rb, Rb, ebb[:, t + 1:t + 2], ps[5][:, :], ALU.mult, ALU.add,
                              ['Rb', 'ebb%d' % (t + 1), 'ps5'], ['Rb'])
                    k.act(snap[:, t - 1, :], Rb, AF.Identity, ['Rb', 'ebb%d' % t], ['snap%d' % (t - 1)],
                          scale=ebb[:, t:t + 1])

                gate(7)
                for pr in reversed(range(8)):
                    ta, tb = 2 * pr + 1, 2 * pr
                    pq = ps[pr % 2]
                    pqn = 'ps%d' % (pr % 2)
                    tp2 = slice(pr * 256, (pr + 1) * 256)
                    k.mm(pq[:, 0:256], [(WA[:, kc, 0:128], xT[:, kc, tp2]) for kc in range(KC)], ['WA', 'xT'], [pqn])
                    k.mm(pq[:, 256:512], [(WA[:, kc, 128:256], xT[:, kc, tp2]) for kc in range(KC)], ['WA', 'xT'], [pqn])
                    if ta + 2 <= NT - 1:
                        bw_state(ta + 2)
                    for t in (ta, tb):
                        u = t - tb
                        par = t % 2
                        tl = slice(t * 128, (t + 1) * 128)
                        pb = ps[4][:, par * 256:(par + 1) * 256]
                        pbn = 'ps4_%d' % par
                        k.mm(pb[:, 0:128], [(sp2[:, u * 256:u * 256 + 128], trif)], ['sp2', 'const'], [pbn])
                        k.mm(pb[:, 128:256], [(sp2[:, u * 256 + 128:u * 256 + 256], trib)], ['sp2', 'const'], [pbn])
                        k.act(eq[par], pb, AF.Exp, [pbn], ['eq%d' % par], bias=lnscale)
                        k.act(ek[par], pb, AF.Exp, [pbn], ['ek%d' % par], scale=-1.0)
                        k.act(ebf[:, t:t + 1], pb[:, 127:128], AF.Exp, [pbn], ['ebf%d' % t])
                        k.act(ebb[:, t:t + 1], pb[:, 128:129], AF.Exp, [pbn], ['ebb%d' % t])
                        c0 = u * 128
                        k.tt(qtf[:, tl], pq[:, c0:c0 + 128], eq[par][:, 0:128], ALU.mult, [pqn, 'eq%d' % par], ['q%d' % t])
                        k.tt(qtb[:, tl], pq[:, c0:c0 + 128], eq[par][:, 128:256], ALU.mult, [pqn, 'eq%d' % par], ['q%d' % t])
                        k.tt(ktf[:, tl], pq[:, 256 + c0:256 + c0 + 128], ek[par][:, 0:128], ALU.mult, [pqn, 'ek%d' % par], ['k%d' % t])
                        k.tt(ktb[:, tl], pq[:, 256 + c0:256 + c0 + 128], ek[par][:, 128:256], ALU.mult, [pqn, 'ek%d' % par], ['k%d' % t])
                    if pr > 0:
                        gate(pr - 1)
                    for t in (ta, tb):
                        tl = slice(t * 128, (t + 1) * 128)
                        vb = vbank[t % 2]
                        k.mm(ps[vb][:, :], [(xT[:, kc, tl], WA[:, kc, 256:768]) for kc in range(KC)], ['WA', 'xT'], ['ps%d' % vb])
                        k.cp(v[:, t, :], ps[vb][:, :], ['ps%d' % vb], ['v%d' % t], eng='act')
                        if t == ta and tb + 2 <= NT - 1:
                            bw_state(tb + 2)
                    for t in (ta, tb):
                        tl = slice(t * 128, (t + 1) * 128)
                        par = t % 2
                        pc = psT[:, par * 256:(par + 1) * 256]
                        pcn = 'psT_%d' % par
                        k.tr(pc[:, 0:128], ktf[:, tl], ['k%d' % t], [pcn])
                        k.tr(pc[:, 128:256], ktb[:, tl], ['k%d' % t], [pcn])
                        k.cp(ktok[:, t, :], pc[:, 0:128], [pcn], ['ktok%d' % t])
                        k.cp(ktmp[par], pc[:, 128:256], [pcn], ['ktmp%d' % par])
                bw_state(1)
                if h < 3:
                    k.dma(WA, gWA_h[j, h + 1].rearrange("p (k f) -> p k f", k=8), [], ['WA'], eng='pool')

                def stage_a_pe(t):
                    tl = slice(t * 128, (t + 1) * 128)
                    par = t % 2
                    po = ps[3] if par == 0 else ps[6]
                    pon = 'ps3' if par == 0 else 'ps6'
                    k.mm(ps[0][:, 0:128], [(ktf[:, tl], qtf[:, tl])], ['k%d' % t, 'q%d' % t], ['ps0'])
                    k.mm(ps[0][:, 128:256], [(ktb[:, tl], qtb[:, tl])], ['k%d' % t, 'q%d' % t], ['ps0'])
                    k.tt(sT[par], ps[0][:, 0:256], maskfb, ALU.mult, ['ps0', 'const'], ['sT%d' % par])
                    if t < NT - 1:
                        k.mm(ps[1][:, :], [(ktok[:, t, :], v[:, t, :])], ['ktok%d' % t, 'v%d' % t], ['ps1'])
                        if t == 0:
                            k.cp(Rf, ps[1][:, :], ['ps1'], ['Rf'])
                        else:
                            k.stt(Rf, Rf, ebf[:, t - 1:t], ps[1][:, :], ALU.mult, ALU.add,
                                  ['Rf', 'ebf%d' % (t - 1), 'ps1'], ['Rf'])
                        k.act(sbf[par], Rf, AF.Identity, ['Rf', 'ebf%d' % t], ['sbf%d' % par], scale=ebf[:, t:t + 1])
                    k.mm(ps[2][:, :], [(xT[:, kc, tl], WZ[:, kc, :]) for kc in range(KC)], ['WZ', 'xT'], ['ps2'])
                    k.cp(etmp, ps[2][:, :], ['ps2'], ['etmp'])
                    k.act(sg, etmp, AF.Exp, ['etmp'], ['sg'], scale=-1.0)
                    k.act(sg, sg, AF.Ln, ['sg'], ['sg'], bias=1.0)
                    k.act(sg, sg, AF.Exp, ['sg'], ['sg'], scale=-1.0)
                    pairs = []
                    rd = ['sT%d' % par, 'v%d' % t, 'q%d' % t]
                    if t > 0:
                        pairs.append((qtf[:, tl], sbf[(t - 1) % 2]))
                        rd.append('sbf%d' % ((t - 1) % 2))
                    pairs.append((sT[par][:, 0:128], v[:, t, :]))
                    if t < NT - 1:
                        pairs.append((qtb[:, tl], snap[:, t, :]))
                        rd.append('snap%d' % t)
                    pairs.append((sT[par][:, 128:256], v[:, t, :]))
                    k.mm(po[:, :], pairs, rd, [pon])
                    P.op('act', lambda e: e.memzero(sm[:, 0:1]), [], ['ss'])
                    k.act(og[par], po[:, :], AF.Square, [pon], ['og%d' % par, 'ss'], accum_out=sm[:, 0:1])
                    k.act(sm[:, 1:2], sm[:, 0:1], AF.Ln, ['ss'], ['lnv'], bias=EPS, scale=1.0 / 512)
                    k.act(sm[:, 2 + par:3 + par], sm[:, 1:2], AF.Exp, ['lnv'], ['rstd%d' % par], scale=-0.5)

                def stage_a_tail(t):
                    par = t % 2
                    po = ps[3] if par == 0 else ps[6]
                    pon = 'ps3' if par == 0 else 'ps6'
                    k.tt(zs, etmp, sg, ALU.mult, ['etmp', 'sg'], ['zs'])
                    k.stt(og[par], po[:, :], sm[:, 2 + par:3 + par], zs, ALU.mult, ALU.mult, [pon, 'rstd%d' % par, 'zs'], ['og%d' % par])

                def stage_b1(t):
                    par = t % 2
                    for c in range(4):
                        k.tr(psT[:, 512 + c * 128:512 + (c + 1) * 128], og[par][:, c * 128:(c + 1) * 128], ['og%d' % par], ['psTb'])
                    k.cp(ogT[par], psT[:, 512:1024].rearrange("p (c t) -> p c t", c=4), ['psTb'], ['ogT%d' % par])

                def stage_b2(t):
                    par = t % 2
                    xr = 'x%d' % t
                    k.mm(ps[4][:, :], [(ogT[par][:, c, :], WO[:, c, 0:512]) for c in range(4)], ['ogT%d' % par, 'WO'], ['ps4_0', 'ps4_1'])
                    k.mm(ps[5][:, :], [(ogT[par][:, c, :], WO[:, c, 512:1024]) for c in range(4)], ['ogT%d' % par, 'WO'], ['ps5'])
                    for half, pb in ((0, 4), (1, 5)):
                        xs = x32[:, t, half * 512:(half + 1) * 512]
                        pbn = ['ps4_0', 'ps4_1'] if pb == 4 else ['ps5']
                        if h == 0:
                            k.stt(xs, xs, ALPHA, ps[pb][:, :], ALU.mult, ALU.add, [xr] + pbn, [xr])
                        else:
                            k.tt(xs, xs, ps[pb][:, :], ALU.add, [xr] + pbn, [xr])

                stage_a_pe(0)
                stage_a_tail(0)
                for t in range(NT):
                    if t + 1 < NT:
                        stage_a_pe(t + 1)
                    stage_b1(t)
                    if t >= 1:
                        stage_b2(t - 1)
                    if t + 1 < NT:
                        stage_a_tail(t + 1)
                stage_b2(NT - 1)

        def mla_layer(j, i, seq):
            A16.reset()
            A32.reset()
            WI = A16.take(8 * 896).rearrange("p (k f) -> p k f", k=8)
            cqn = A16.take(3 * S).rearrange("p (k t) -> p k t", k=3)
            ckvn = A16.take(2 * S).rearrange("p (k t) -> p k t", k=2)
            Kpp = A16.take(S)
            CC = A16.take(S)
            SS = A16.take(S)
            qn = A16.take(S)
            qr = A16.take(S)
            kn = A16.take(S)
            vv = A16.take(NT * 128)
            silu = A16.take(S)
            ogT = A16.take(S)
            WHb = [A16.take(3328), A16.take(3328)]
            PT = [A16.take(512), A16.take(512), A16.take(512)]
            sq = [A16.take(512), A16.take(512), A16.take(512)]
            sq3 = A16.take(512)
            f0 = A32.take(512)
            f1 = A32.take(512)
            f2 = A32.take(512)
            f3 = A32.take(512)
            rstd = A32.take(512)
            rec = A32.take(512)
            qg = A32.take(3)
            kg = A32.take(2)
            sc = 192 ** -0.5
            TWO_PI = 2.0 * math.pi

            if k.prefetch_only:
                k.dma(WI, mWI_h[j].rearrange("p (k f) -> p k f", k=8), [], ['WI'], eng='pool')
                k.dma(qg, mqg_h[j], [], ['qg'])
                k.dma(kg, mkg_h[j], [], ['kg'])
                k.dma(WHb[0], mWH_h[j, 0], [], ['WH0'], eng='pool')
                return
            f0i = f0.bitcast(I32)
            f3i = f3.bitcast(I32)
            for g in range(4):
                tg = slice(g * 512, (g + 1) * 512)
                for (c0, nch, dst, gcol, gname, width) in ((0, 3, cqn, qg, 'qg', 384.0), (384, 2, ckvn, kg, 'kg', 256.0)):
                    for c in range(nch):
                        k.mm(ps[c][:, :], [(WI[:, kc, c0 + c * 128:c0 + (c + 1) * 128], xT[:, kc, tg]) for kc in range(KC)],
                             ['WI', 'xT'], ['ps%d' % c])
                    for c in range(nch):
                        k.act(sq[c], ps[c][:, :], AF.Square, ['ps%d' % c], ['sq%d' % c])
                        k.mm(ps[3][:, :], [(onesb, sq[c])], ['sq%d' % c, 'const'], ['ps3'], first=(c == 0), last=(c == nch - 1))
                    k.act(rstd, ps[3][:, :], AF.Ln, ['ps3'], ['rstd'], bias=EPS, scale=1.0 / width)
                    k.act(rstd, rstd, AF.Exp, ['rstd'], ['rstd'], scale=-0.5)
                    for c in range(nch):
                        k.stt(dst[:, c, tg], ps[c][:, :], gcol[:, c:c + 1], rstd, ALU.mult, ALU.mult,
                              ['ps%d' % c, gname, 'rstd'], ['lowrank'])
                k.mm(ps[4][:, :], [(WI[:, kc, 640:768], xT[:, kc, tg]) for kc in range(KC)], ['WI', 'xT'], ['ps4'])
                k.mm(ps[5][:, :], [(WI[:, kc, 768:896], xT[:, kc, tg]) for kc in range(KC)], ['WI', 'xT'], ['ps5'])
                k.dma(f0i, pos_h[seq, 0:1, tg].partition_broadcast(128), [], ['f0'])
                k.cp(f1, f0i, ['f0'], ['f1'])
                k.ts(f1, f1, invf, 1.0 / TWO_PI, ALU.mult, ALU.mult, ['f1', 'const'], ['f1'])
                for tab, shift in ((CC, 0.25), (SS, shiftS)):
                    k.ts(f2, f1, shift, None, ALU.add, None, ['f1', 'const'], ['f2'])
                    k.cp(f3i, f2, ['f2'], ['f3'])
                    k.cp(f0, f3i, ['f3'], ['f0'])
                    k.tt(f2, f2, f0, ALU.subtract, ['f2', 'f0'], ['f2'])
                    k.act(tab[:, tg], f2, AF.Sin, ['f2'], ['tab'], scale=TWO_PI)
                k.tt(f0, ps[4][:, :], CC[:, tg], ALU.mult, ['ps4', 'tab'], ['f0'])
                k.tt(f1, ps[5][:, :], SS[:, tg], ALU.mult, ['ps5', 'tab'], ['f1'])
                k.tt(Kpp[:, tg], f0, f1, ALU.add, ['f0', 'f1'], ['Kpp'])

            def proj_chunks(h):
                WH = WHb[h % 2]
                whn = 'WH%d' % (h % 2)
                wuq = WH[:, 0:768].rearrange("p (k f) -> p k f", k=3)
                wukv = WH[:, 768:1280].rearrange("p (k f) -> p k f", k=2)
                wz = WH[:, 1280:2304].rearrange("p (k f) -> p k f", k=8)
                out = []
                for g in range(4):
                    tg = slice(g * 512, (g + 1) * 512)

                    def c_qn(tg=tg):
                        k.mm(ps[0][:, :], [(wuq[:, kc, 0:128], cqn[:, kc, tg]) for kc in range(3)], [whn, 'lowrank'], ['ps0'])
                        k.cp(qn[:, tg], ps[0][:, :], ['ps0'], ['qn'], eng='act')

                    def c_qr(tg=tg):
                        k.mm(ps[1][:, :], [(wuq[:, kc, 128:256], cqn[:, kc, tg]) for kc in range(3)], [whn, 'lowrank'], ['ps1'])
                        k.tt(qr[0:64, tg], ps[1][0:64, :], CC[0:64, tg], ALU.mult, ['ps1', 'tab'], ['qr'])
                        k.tt(qr[64:128, tg], ps[1][64:128, :], SS[64:128, tg], ALU.mult, ['ps1', 'tab'], ['qr'])

                    def c_kn(tg=tg):
                        k.mm(ps[0][:, :], [(wukv[:, kc, 0:128], ckvn[:, kc, tg]) for kc in range(2)], [whn, 'lowrank'], ['ps0'])
                        k.cp(kn[:, tg], ps[0][:, :], ['ps0'], ['kn'], eng='act')

                    def c_z(tg=tg):
                        k.mm(ps[2][:, :], [(wz[:, kc, :], xT[:, kc, tg]) for kc in range(KC)], [whn, 'xT'], ['ps2'])
                        k.act(f0, ps[2][:, :], AF.Exp, ['ps2'], ['f0'], scale=-1.0)
                        k.act(f0, f0, AF.Ln, ['f0'], ['f0'], bias=1.0)
                        k.act(f0, f0, AF.Exp, ['f0'], ['f0'], scale=-1.0)
                        k.tt(silu[:, tg], ps[2][:, :], f0, ALU.mult, ['ps2', 'f0'], ['silu'])

                    def c_v(g=g):
                        for u in range(4):
                            tl = slice((4 * g + u) * 128, (4 * g + u + 1) * 128)
                            k.mm(ps[1][:, u * 128:(u + 1) * 128], [(ckvn[:, kc, tl], wukv[:, kc, 128:256]) for kc in range(2)],
                                 [whn, 'lowrank'], ['ps1'])
                        k.cp(vv[:, g * 512:(g + 1) * 512], ps[1][:, :], ['ps1'], ['vv'])

                    out += [c_qn, c_qr, c_z, c_kn, c_v]
                return out

            for c in proj_chunks(0):
                c()
            for h in range(16):
                WH = WHb[h % 2]
                whn = 'WH%d' % (h % 2)
                if h < 15:
                    k.dma(WHb[(h + 1) % 2], mWH_h[j, h + 1], [], ['WH%d' % ((h + 1) % 2)], eng='pool')
                wo = WH[:, 2304:3328]
                blocks = [(g, kt) for g in range(4) for kt in range(NT)]
                Sb = [5, 6, 0]
                accD = [rstd, f2]
                accDn = ['rstd', 'f2']

                def emit_S(bi):
                    g, kt = blocks[bi]
                    qs = slice(g * 512, (g + 1) * 512)
                    ktl = slice(kt * 128, (kt + 1) * 128)
                    b = bi % 3
                    pS = ps[Sb[b]]
                    k.mm(pS[:, :], [(kn[:, ktl], qn[:, qs]), (Kpp[:, ktl], qr[:, qs])], ['kn', 'qn', 'Kpp', 'qr'], ['ps%d' % Sb[b]])
                    k.act(PT[b], pS[:, :], AF.Exp, ['ps%d' % Sb[b]], ['PT%d' % b], scale=sc)

                def finish(g):
                    qs = slice(g * 512, (g + 1) * 512)
                    pO, pD = (3, 4) if g % 2 == 0 else (1, 2)
                    k.mm(ps[pD][:, :], [(onesb, sq[0]), (onesb, sq[1])], ['const', 'sq0', 'sq1'], ['ps%d' % pD], first=False, last=True)
                    k.act(rec, ps[pD][:, :], AF.Ln, ['ps%d' % pD], ['rec'])
                    k.act(rec, rec, AF.Exp, ['rec'], ['rec'], scale=-1.0)
                    k.tt(f1, ps[pO][:, :], rec, ALU.mult, ['ps%d' % pO, 'rec'], ['f1'])
                    k.tt(ogT[:, qs], f1, silu[:, qs], ALU.mult, ['f1', 'silu'], ['ogT%d' % g])

                emit_S(0)
                emit_S(1)
                pending = []
                for bi, (g, kt) in enumerate(blocks):
                    if bi + 2 < len(blocks):
                        emit_S(bi + 2)
                    ktl = slice(kt * 128, (kt + 1) * 128)
                    b = bi % 3
                    pO, pD = (3, 4) if g % 2 == 0 else (1, 2)
                    k.mm(ps[pO][:, :], [(vv[:, ktl], PT[b])], ['vv', 'PT%d' % b], ['ps%d' % pO], first=(kt == 0), last=(kt == NT - 1))
                    if kt % 2 == 0:
                        k.mm(ps[pD][:, :], [(onesb, PT[b])], ['const', 'PT%d' % b], ['ps%d' % pD], first=(kt == 0), last=False)
                    else:
                        acc, an = accD[g % 2], accDn[g % 2]
                        if kt == 1:
                            k.cp(acc, PT[b], ['PT%d' % b], [an])
                        else:
                            k.tt(acc, acc, PT[b], ALU.add, [an, 'PT%d' % b], [an])
                    if pending and pending[0][1] == bi:
                        finish(pending.pop(0)[0])
                    if kt == NT - 1:
                        k.cp(sq[0], accD[g % 2], [accDn[g % 2]], ['sq0'])
                        k.tt(sq[1], accD[g % 2], sq[0], ALU.subtract, [accDn[g % 2], 'sq0'], ['sq1'])
                        pending.append((g, bi + 4))
                while pending:
                    finish(pending.pop(0)[0])
                nxt = proj_chunks(h + 1) if h < 15 else []
                for t in range(NT):
                    tl = slice(t * 128, (t + 1) * 128)
                    xr = 'x%d' % t
                    for half in (0, 1):
                        pb = (5 + half) if t % 2 == 0 else (3 + half)
                        k.mm(ps[pb][:, :], [(ogT[:, tl], wo[:, half * 512:(half + 1) * 512])], ['ogT%d' % (t // 4), whn], ['ps%d' % pb])
                        xs = x32[:, t, half * 512:(half + 1) * 512]
                        xrh = xr + '_%d' % half
                        if h == 0:
                            k.stt(xs, xs, ALPHA, ps[pb][:, :], ALU.mult, ALU.add, [xrh, 'ps%d' % pb], [xrh])
                        else:
                            k.tt(xs, xs, ps[pb][:, :], ALU.add, [xrh, 'ps%d' % pb], [xrh])
                    if nxt:
                        nxt.pop(0)()
                    if nxt and t % 4 == 3:
                        nxt.pop(0)()
                while nxt:
                    nxt.pop(0)()

        def prefetch(kind, j):
            k.prefetch_only = True
            if kind == 'gla':
                gla_layer(j, None, None)
            else:
                mla_layer(j, None, None)
            k.prefetch_only = False

        for seq in range(nseq):
            P.barrier()
            if seq == 0:
                k.xbs = [a16t[:, N16 - 2 * D:N16 - D], a16t[:, N16 - D:N16]]
                prefetch(*layers[0][:2])
                for t in range(NT):
                    k.dma(x32[:, t, :], x_h[seq, t * 128:(t + 1) * 128, :], [], ['x%d' % t])
                    make_xT(t)
            for li, (kind, j, i) in enumerate(layers):
                P.barrier()
                k.prefetch_only = False
                if kind == 'gla':
                    gla_layer(j, i, seq)
                else:
                    mla_layer(j, i, seq)
                P.barrier()
                if li + 1 < len(layers):
                    nxt_layer = layers[li + 1][:2]
                elif seq + 1 < nseq:
                    nxt_layer = layers[0][:2]
                else:
                    nxt_layer = None
                layer_norm_phase(i, seq, last=(li == len(layers) - 1), nxt_layer=nxt_layer)
        P.barrier()
        P.emit()
    return nc


def host_consts():
    c = np.zeros((128, 7 * 128), np.float32)
    c[:, 0:128] = np.eye(128)
    jj = np.arange(128)[:, None]
    ii = np.arange(128)[None, :]
    c[:, 128:256] = np.where(jj <= ii, -1.0 / 16, 0.0)
    c[:, 256:384] = np.where(jj >= ii, -1.0 / 16, 0.0)
    c[:, 384:512] = (jj <= ii)
    c[:, 512:640] = (jj >= ii)
    c[:, 640:768] = 1.0
    inv_freq = (1.0 / (10000.0 ** (np.arange(0, 64, 2, dtype=np.float32) / 64))).astype(np.float32)
    c[:, 768] = inv_freq[np.arange(128) % 32]
    c[:, 769] = np.where((np.arange(128) // 32) % 2 == 0, 0.5, 0.0)
    return c


def host_layout(inp):
    f = np.float32
    d = {}
    d["ln_g"] = np.ascontiguousarray(inp["ln_g"], f).reshape(DEPTH, 1, D)
    d["ln_b"] = np.ascontiguousarray(inp["ln_b"], f).reshape(DEPTH, 1, D)
    d["consts"] = host_consts()
    w_in = np.asarray(inp["gla_w_in"], f)
    WA = np.zeros((2, 4, 128, 8, 768), f)
    WZ = np.zeros((2, 4, 128, 8, 512), f)
    for h in range(4):
        blk = np.concatenate([w_in[:, :, h * 128:(h + 1) * 128], w_in[:, :, 512 + h * 128:512 + (h + 1) * 128],
                              w_in[:, :, 1024 + h * 512:1024 + (h + 1) * 512]], axis=2)
        WA[:, h] = blk.reshape(2, 8, 128, 768).transpose(0, 2, 1, 3)
        zb = w_in[:, :, 3072 + h * 512:3072 + (h + 1) * 512]
        WZ[:, h] = zb.reshape(2, 8, 128, 512).transpose(0, 2, 1, 3)
    d["gla_WA"] = WA.reshape(2, 4, 128, 8 * 768)
    d["gla_WZ"] = WZ.reshape(2, 4, 128, 8 * 512)
    wo = np.asarray(inp["gla_w_out"], f)
    d["gla_WO"] = np.ascontiguousarray(wo.reshape(2, 4, 4, 128, 1024).transpose(0, 1, 3, 2, 4)).reshape(2, 4, 128, 4096)
    d["gla_WGL"] = np.ascontiguousarray(w_in[:, :, 5120:5152].reshape(2, 8, 128, 32).transpose(0, 2, 1, 3)).reshape(2, 128, 256)
    wgate = np.asarray(inp["gla_w_gate"], f)
    bgate = np.asarray(inp["gla_b_gate"], f)
    wg = np.zeros((2, 4, 33, 256), f)
    for h in range(4):
        wg[:, h, 0:16, 0:128] = wgate[:, 0, :, h * 128:(h + 1) * 128]
        wg[:, h, 16:32, 128:256] = wgate[:, 1, :, h * 128:(h + 1) * 128]
        wg[:, h, 32, 0:128] = bgate[:, 0, h * 128:(h + 1) * 128]
        wg[:, h, 32, 128:256] = bgate[:, 1, h * 128:(h + 1) * 128]
    d["gla_wg"] = wg
    d["gla_gn"] = np.ascontiguousarray(np.asarray(inp["gla_gn_g"], f).reshape(2, 16, 128).transpose(0, 2, 1))
    mw = np.asarray(inp["mla_w_in"], f)
    kr = mw[:, :, 640:704]
    krP = np.concatenate([kr[:, :, 32:64], kr[:, :, 0:32]], axis=2)
    wi = np.concatenate([mw[:, :, 0:640], kr, kr, krP, krP], axis=2)
    d["mla_WI"] = np.ascontiguousarray(wi.reshape(2, 8, 128, 896).transpose(0, 2, 1, 3)).reshape(2, 128, 8 * 896)
    uq = np.asarray(inp["mla_w_uq"], f).reshape(2, 3, 128, 16, 192)
    ukv = np.asarray(inp["mla_w_ukv"], f).reshape(2, 2, 128, 16, 256)
    mz = mw[:, :, 704:2752].reshape(2, 8, 128, 16, 128)
    mo = np.asarray(inp["mla_w_out"], f).reshape(2, 16, 128, 1024)
    WH = np.zeros((2, 16, 128, 3328), f)
    for h in range(16):
        qn = uq[:, :, :, h, 0:128]
        qr = uq[:, :, :, h, 128:192]
        qrP = np.concatenate([qr[..., 32:64], qr[..., 0:32]], axis=-1)
        blkq = np.concatenate([qn, qr, qrP], axis=-1)
        WH[:, h, :, 0:768] = blkq.transpose(0, 2, 1, 3).reshape(2, 128, 768)
        WH[:, h, :, 768:1280] = ukv[:, :, :, h, :].transpose(0, 2, 1, 3).reshape(2, 128, 512)
        WH[:, h, :, 1280:2304] = mz[:, :, :, h, :].transpose(0, 2, 1, 3).reshape(2, 128, 1024)
        WH[:, h, :, 2304:3328] = mo[:, h]
    d["mla_WH"] = WH
    d["mla_qg"] = np.ascontiguousarray(np.asarray(inp["mla_q_norm_g"], f).reshape(2, 3, 128).transpose(0, 2, 1))
    d["mla_kg"] = np.ascontiguousarray(np.asarray(inp["mla_kv_norm_g"], f).reshape(2, 2, 128).transpose(0, 2, 1))
    return d


ALL_LAYERS = [('gla', 0, 0), ('mla', 0, 1), ('gla', 1, 2), ('mla', 1, 3)]


FUSED = True


def _run(x, pos, shared, layers):
    nc = build_program(NSEQ, layers)
    in_maps = []
    for c in range(NCORES):
        m = dict(shared)
        m["x"] = np.ascontiguousarray(x[c * NSEQ:(c + 1) * NSEQ])
        m["pos"] = np.ascontiguousarray(pos[c * NSEQ:(c + 1) * NSEQ]).reshape(NSEQ, 1, S)
        in_maps.append(m)
    res = run_bass_kernel_spmd(nc, in_maps, core_ids=list(range(NCORES)))
    return np.concatenate([np.asarray(r["out"], np.float32) for r in res.results], axis=0)


def kernel(**inputs):
    x = np.asarray(inputs["x"], np.float32)
    pos = np.asarray(inputs["positions"], np.int32)
    shared = host_layout(inputs)
    if FUSED:
        return _run(x, pos, shared, ALL_LAYERS)
    for lay in ALL_LAYERS:
        x = _run(x, pos, shared, [lay])
    return x
```
